# Optimizing a Trainium2 kernel written in Bass

```python
import jax
import jax.numpy as jnp
from jax import lax
import numpy as np

D_MODEL = 1024
BATCH = 4
SEQ = 8192
DEPTH = 2
DEC_BATCH = 16
DEC_SEQ = 64
PAST_LEN = 4096

CHUNK = 64
D_CONV = D_MODEL // 4
D_GMLP = D_MODEL // 4
D_ATT = D_MODEL // 2
D_MIX = D_CONV + D_GMLP + D_ATT
CONV_WIDTH = 31
CONV_PAD = CONV_WIDTH - 1
GM_HEAD_DIM = 64
GM_HEADS = D_GMLP // GM_HEAD_DIM
GM_CHUNK = 128
HEAD_DIM = 64
ATT_HEADS = D_ATT // HEAD_DIM
BAND_CHUNKS = 8
BAND_PAST = BAND_CHUNKS * CHUNK
BAND_LEN = BAND_PAST + CHUNK
REL_CLIP = 128
N_REL = 2 * REL_CLIP + 1
D_IN = 2 * D_CONV + 2 * D_GMLP + 3 * D_ATT
D_FF = ((8 * D_MODEL // 3 + 127) // 128) * 128
EPS = 1e-6

kernel_name = 'hybrid_streaming_encoder_step'


def rms_norm(x, g):
    xf = x.astype(jnp.float32)
    y = xf * lax.rsqrt(jnp.mean(xf * xf, axis=-1, keepdims=True) + EPS)
    return (y * g.astype(jnp.float32)).astype(x.dtype)


def layer_norm(x, g, b):
    xf = x.astype(jnp.float32)
    mu = jnp.mean(xf, axis=-1, keepdims=True)
    var = jnp.mean(jnp.square(xf - mu), axis=-1, keepdims=True)
    y = (xf - mu) * lax.rsqrt(var + EPS)
    return (y * g.astype(jnp.float32) + b.astype(jnp.float32)).astype(x.dtype)


def swiglu_ffn(x, g, w_gate, w_up, w_down):
    h = rms_norm(x, g)
    return (jax.nn.silu(h @ w_gate) * (h @ w_up)) @ w_down


def begin_mixing(x, p):
    B, T, _ = x.shape
    x = x + 0.5 * swiglu_ffn(x, p['ffn1_norm'], p['ffn1_w_gate'], p['ffn1_w_up'], p['ffn1_w_down'])
    z = rms_norm(x, p['mix_norm']) @ p['w_in']
    o1 = 2 * D_CONV
    o2 = o1 + 2 * D_GMLP
    o3 = o2 + D_ATT
    o4 = o3 + D_ATT
    zc, zg = z[..., :o1], z[..., o1:o2]
    zq, zk, zv = z[..., o2:o3], z[..., o3:o4], z[..., o4:]
    glu = zc[..., :D_CONV] * jax.nn.sigmoid(zc[..., D_CONV:])
    uv = jax.nn.gelu(zg)
    u = uv[..., :D_GMLP].reshape(B, T, GM_HEADS, GM_HEAD_DIM)
    vg = layer_norm(uv[..., D_GMLP:], p['gmlp_ln_g'], p['gmlp_ln_b']).reshape(B, T, GM_HEADS, GM_HEAD_DIM)
    q = rms_norm(zq.reshape(B, T, ATT_HEADS, HEAD_DIM), p['q_norm'])
    k = rms_norm(zk.reshape(B, T, ATT_HEADS, HEAD_DIM), p['k_norm'])
    v = zv.reshape(B, T, ATT_HEADS, HEAD_DIM)
    return x, glu, u, vg, q, k, v


def conv_branch(glu_padded, p):
    y = lax.conv_general_dilated(glu_padded, p['conv_w'][:, None, :], (1,), 'VALID',
                                 dimension_numbers=('NWC', 'WIO', 'NWC'),
                                 feature_group_count=D_CONV) + p['conv_b']
    return jax.nn.silu(layer_norm(y, p['conv_ln_g'], p['conv_ln_b']))


def gmlp_spatial(v, ws, bs):
    tc = v.shape[2]
    w = jnp.tril(ws[:, :tc, :tc])
    return jnp.einsum('hts,bnshd->bnthd', w, v) + bs[:, :tc].T[None, None, :, :, None]


def band_attention(q, k, v, q_pos, k_pos, rel_bias):
    rel = q_pos[:, None] - k_pos[None, :]
    bias = rel_bias[:, jnp.clip(rel, -REL_CLIP, REL_CLIP) + REL_CLIP].astype(jnp.float32)
    qc = q_pos[:, None] // CHUNK
    kc = k_pos[None, :] // CHUNK
    allowed = (k_pos[None, :] >= 0) & (kc <= qc) & (kc >= qc - BAND_CHUNKS)
    s = jnp.einsum('bqhd,bkhd->bhqk', q, k).astype(jnp.float32) * (HEAD_DIM ** -0.5) + bias
    s = jnp.where(allowed, s, -1e30)
    pr = jax.nn.softmax(s, axis=-1)
    return jnp.einsum('bhqk,bkhd->bqhd', pr.astype(v.dtype), v)


def prompt_band_attention(q, k, v, rel_bias):
    B, S, H, Dh = q.shape
    n_chunks = S // CHUNK
    pad = ((0, 0), (BAND_PAST, 0), (0, 0), (0, 0))
    kp = jnp.pad(k, pad)
    vp = jnp.pad(v, pad)

    def one_chunk(c):
        start = c * CHUNK
        qs = lax.dynamic_slice_in_dim(q, start, CHUNK, axis=1)
        ks = lax.dynamic_slice_in_dim(kp, start, BAND_LEN, axis=1)
        vs = lax.dynamic_slice_in_dim(vp, start, BAND_LEN, axis=1)
        q_pos = start + jnp.arange(CHUNK)
        k_pos = start - BAND_PAST + jnp.arange(BAND_LEN)
        return band_attention(qs, ks, vs, q_pos, k_pos, rel_bias)

    out = lax.map(one_chunk, jnp.arange(n_chunks))
    return out.transpose(1, 0, 2, 3, 4).reshape(B, S, H, Dh)


def finish_layer(x, conv_out, gm_out, att_out, p):
    B, T, _ = x.shape
    mixed = jnp.concatenate([conv_out, gm_out.reshape(B, T, D_GMLP), att_out.reshape(B, T, D_ATT)], axis=-1)
    x = x + mixed @ p['w_out']
    return x + 0.5 * swiglu_ffn(x, p['ffn2_norm'], p['ffn2_w_gate'], p['ffn2_w_up'], p['ffn2_w_down'])


def prompt_layer(x, p):
    B, S, _ = x.shape
    x, glu, u, vg, q, k, v = begin_mixing(x, p)
    conv_out = conv_branch(jnp.pad(glu, ((0, 0), (CONV_PAD, 0), (0, 0))), p)
    s = gmlp_spatial(vg.reshape(B, S // GM_CHUNK, GM_CHUNK, GM_HEADS, GM_HEAD_DIM), p['gmlp_ws'], p['gmlp_b'])
    gm_out = u * s.reshape(B, S, GM_HEADS, GM_HEAD_DIM)
    att_out = prompt_band_attention(q, k, v, p['rel_bias'])
    x = finish_layer(x, conv_out, gm_out, att_out, p)
    keep = min(BAND_PAST, S)
    return x, glu[:, -CONV_PAD:], k[:, -keep:], v[:, -keep:]


def sample_layer(x, conv_buf, k_cache, v_cache, p):
    B, T, _ = x.shape
    L = k_cache.shape[1]
    x, glu, u, vg, q, k, v = begin_mixing(x, p)
    conv_in = jnp.concatenate([conv_buf.astype(glu.dtype), glu], axis=1)
    conv_out = conv_branch(conv_in, p)
    gm_out = u * gmlp_spatial(vg[:, None], p['gmlp_ws'], p['gmlp_b'])[:, 0]
    q_pos = PAST_LEN + jnp.arange(T)
    k_pos = PAST_LEN + jnp.arange(-L, T)
    k_all = jnp.concatenate([k_cache.astype(k.dtype), k], axis=1)
    v_all = jnp.concatenate([v_cache.astype(v.dtype), v], axis=1)
    att_out = band_attention(q, k_all, v_all, q_pos, k_pos, p['rel_bias'])
    x = finish_layer(x, conv_out, gm_out, att_out, p)
    return x, conv_in[:, -CONV_PAD:], k, v, vg


def setup_inputs(seed: int = 0) -> dict:
    key = jax.random.key(seed)
    ks = jax.random.split(key, 32)
    att_cache = min(BAND_PAST, PAST_LEN)

    def nrm(k, shape, scale):
        return jax.random.normal(k, shape, jnp.float32) * scale

    return {
        'x_prompt': nrm(ks[0], (BATCH, SEQ, D_MODEL), 1.0),
        'x_sample': nrm(ks[1], (DEC_BATCH, DEC_SEQ, D_MODEL), 1.0),
        'cache_conv': nrm(ks[2], (DEPTH, DEC_BATCH, CONV_PAD, D_CONV), 0.5),
        'cache_k': nrm(ks[3], (DEPTH, DEC_BATCH, att_cache, ATT_HEADS, HEAD_DIM), 1.0),
        'cache_v': nrm(ks[4], (DEPTH, DEC_BATCH, att_cache, ATT_HEADS, HEAD_DIM), 1.0),
        'ffn1_norm': 1.0 + nrm(ks[5], (DEPTH, D_MODEL), 0.01),
        'ffn1_w_gate': nrm(ks[6], (DEPTH, D_MODEL, D_FF), D_MODEL ** -0.5),
        'ffn1_w_up': nrm(ks[7], (DEPTH, D_MODEL, D_FF), D_MODEL ** -0.5),
        'ffn1_w_down': nrm(ks[8], (DEPTH, D_FF, D_MODEL), D_FF ** -0.5),
        'mix_norm': 1.0 + nrm(ks[9], (DEPTH, D_MODEL), 0.01),
        'w_in': nrm(ks[10], (DEPTH, D_MODEL, D_IN), D_MODEL ** -0.5),
        'conv_w': nrm(ks[11], (DEPTH, CONV_WIDTH, D_CONV), CONV_WIDTH ** -0.5),
        'conv_b': nrm(ks[12], (DEPTH, D_CONV), 0.01),
        'conv_ln_g': 1.0 + nrm(ks[13], (DEPTH, D_CONV), 0.01),
        'conv_ln_b': nrm(ks[14], (DEPTH, D_CONV), 0.01),
        'gmlp_ln_g': 1.0 + nrm(ks[15], (DEPTH, D_GMLP), 0.01),
        'gmlp_ln_b': nrm(ks[16], (DEPTH, D_GMLP), 0.01),
        'gmlp_ws': nrm(ks[17], (DEPTH, GM_HEADS, GM_CHUNK, GM_CHUNK), GM_CHUNK ** -0.5),
        'gmlp_b': 1.0 + nrm(ks[18], (DEPTH, GM_HEADS, GM_CHUNK), 0.1),
        'q_norm': 1.0 + nrm(ks[19], (DEPTH, HEAD_DIM), 0.01),
        'k_norm': 1.0 + nrm(ks[20], (DEPTH, HEAD_DIM), 0.01),
        'rel_bias': nrm(ks[21], (DEPTH, ATT_HEADS, N_REL), 0.1),
        'w_out': nrm(ks[22], (DEPTH, D_MIX, D_MODEL), D_MIX ** -0.5),
        'ffn2_norm': 1.0 + nrm(ks[23], (DEPTH, D_MODEL), 0.01),
        'ffn2_w_gate': nrm(ks[24], (DEPTH, D_MODEL, D_FF), D_MODEL ** -0.5),
        'ffn2_w_up': nrm(ks[25], (DEPTH, D_MODEL, D_FF), D_MODEL ** -0.5),
        'ffn2_w_down': nrm(ks[26], (DEPTH, D_FF, D_MODEL), D_FF ** -0.5),
    }


def reference(x_prompt, x_sample, cache_conv, cache_k, cache_v,
              ffn1_norm, ffn1_w_gate, ffn1_w_up, ffn1_w_down,
              mix_norm, w_in, conv_w, conv_b, conv_ln_g, conv_ln_b,
              gmlp_ln_g, gmlp_ln_b, gmlp_ws, gmlp_b,
              q_norm, k_norm, rel_bias, w_out,
              ffn2_norm, ffn2_w_gate, ffn2_w_up, ffn2_w_down):
    xp = x_prompt
    xs = x_sample
    p_conv, p_k, p_v = [], [], []
    s_conv, s_k, s_v, s_gv = [], [], [], []
    for l in range(DEPTH):
        p = {
            'ffn1_norm': ffn1_norm[l], 'ffn1_w_gate': ffn1_w_gate[l], 'ffn1_w_up': ffn1_w_up[l],
            'ffn1_w_down': ffn1_w_down[l], 'mix_norm': mix_norm[l], 'w_in': w_in[l],
            'conv_w': conv_w[l], 'conv_b': conv_b[l], 'conv_ln_g': conv_ln_g[l], 'conv_ln_b': conv_ln_b[l],
            'gmlp_ln_g': gmlp_ln_g[l], 'gmlp_ln_b': gmlp_ln_b[l], 'gmlp_ws': gmlp_ws[l], 'gmlp_b': gmlp_b[l],
            'q_norm': q_norm[l], 'k_norm': k_norm[l], 'rel_bias': rel_bias[l], 'w_out': w_out[l],
            'ffn2_norm': ffn2_norm[l], 'ffn2_w_gate': ffn2_w_gate[l], 'ffn2_w_up': ffn2_w_up[l],
            'ffn2_w_down': ffn2_w_down[l],
        }
        xp, pc, pk, pv = prompt_layer(xp, p)
        xs, sc, sk, sv, sg = sample_layer(xs, cache_conv[l], cache_k[l], cache_v[l], p)
        p_conv.append(pc)
        p_k.append(pk)
        p_v.append(pv)
        s_conv.append(sc)
        s_k.append(sk)
        s_v.append(sv)
        s_gv.append(sg)
    prompt_conv_state = jnp.stack(p_conv)
    prompt_k = jnp.stack(p_k)
    prompt_v = jnp.stack(p_v)
    sample_conv_state = jnp.stack(s_conv)
    sample_k_new = jnp.stack(s_k)
    sample_v_new = jnp.stack(s_v)
    sample_gmlp_v = jnp.stack(s_gv)
    return (xp, xs, prompt_conv_state, prompt_k, prompt_v, sample_conv_state, sample_k_new, sample_v_new, sample_gmlp_v)
```

```python
import contextlib
import numpy as np
import concourse.bass as bass
import concourse.mybir as mybir
from concourse.bass_utils import run_bass_kernel_spmd

F32 = mybir.dt.float32
BF16 = mybir.dt.bfloat16
AF = mybir.ActivationFunctionType
ALU = mybir.AluOpType
AX = mybir.AxisListType

L = 2
NK = 8
NF = 22
NSLAB_A = 58
NSLAB_B = 16
EPS = 1e-6
NTI_FULL = 9
GELU_C = 1.5957691216057308


class Buf:
    __slots__ = ("name", "last_w", "readers")

    def __init__(self, name):
        self.name = name
        self.last_w = None
        self.readers = []


class DmaSem:
    def __init__(self, handle, key):
        self.handle = handle
        self.key = key
        self.count = 0


class Sched:
    ENG = ("pe", "act", "dve", "pool", "sp")

    def __init__(self, nc):
        self.nc = nc
        self.items = {e: [] for e in self.ENG}
        self.cnt = {e: 0 for e in self.ENG}
        self.seen = {e: {} for e in self.ENG}
        self.semh = {}
        for e in self.ENG:
            self.semh["p_" + e] = nc.alloc_semaphore(name="p_" + e)

    def dsem(self, name):
        h = self.nc.alloc_semaphore(name=name)
        self.semh[name] = h
        return DmaSem(h, name)

    def _waits(self, eng, reads, writes):
        w = {}

        def add(t, raw):
            if t is None:
                return
            k, v = t
            if k == "p_" + eng and not raw:
                return
            if w.get(k, 0) < v:
                w[k] = v
        for b in reads:
            add(b.last_w, True)
        for b in writes:
            add(b.last_w, False)
            for t in b.readers:
                add(t, False)
        need = []
        seen = self.seen[eng]
        for k, v in w.items():
            if seen.get(k, 0) < v:
                seen[k] = v
                need.append((k, v))
        return need

    def op(self, eng, meth, kw, reads=(), writes=(), signal=True):
        need = self._waits(eng, reads, writes)
        if signal:
            self.cnt[eng] += 1
            t = ("p_" + eng, self.cnt[eng])
        else:
            t = ("p_" + eng, self.cnt[eng] + 1)
        self.items[eng].append((need, (meth, kw), ("p_" + eng, 1) if signal else None))
        for b in reads:
            if len(b.readers) > 64:
                b.readers = _compress(b.readers)
            b.readers.append(t)
        for b in writes:
            b.last_w = t
            b.readers = []
        return t

    def dma(self, eng, kw, sem, reads=(), writes=()):
        need = self._waits(eng, reads, writes)
        sem.count += 16
        t = (sem.key, sem.count)
        self.items[eng].append((need, ("dma_start", kw), (sem.key, 16)))
        for b in reads:
            b.readers.append(t)
        for b in writes:
            b.last_w = t
            b.readers = []
        return t

    def wait_all(self, eng, tickets):
        need = []
        seen = self.seen[eng]
        for (k, v) in _compress(tickets):
            if seen.get(k, 0) < v:
                seen[k] = v
                need.append((k, v))
        self.items[eng].append((need, None, None))

    def emit(self, block):
        def mk(e):
            items = self.items[e]

            def body(engh):
                for need, fn, sig in items:
                    for k, v in need:
                        engh.wait_ge(self.semh[k], v)
                    if fn is not None:
                        ins = getattr(engh, fn[0])(**fn[1])
                        if sig is not None:
                            ins.then_inc(self.semh[sig[0]], sig[1])
            return body
        block.tensor(mk("pe"))
        block.scalar(mk("act"))
        block.vector(mk("dve"))
        block.gpsimd(mk("pool"))
        block.sync(mk("sp"))


class _Stop(Exception):
    pass


def _compress(ts):
    m = {}
    for k, v in ts:
        if m.get(k, 0) < v:
            m[k] = v
    return list(m.items())


def build_program(NTI, dbg=None):
    nc = bass.Bass("TRN2", target_bir_lowering=False)
    S = Sched(nc)
    dbg = dbg or {}

    def din(name, shape):
        return nc.dram_tensor(name, shape, F32, kind="ExternalInput").ap()

    def dout(name, shape):
        return nc.dram_tensor(name, shape, F32, kind="ExternalOutput").ap()

    xp = din("xp", [NTI * 512, 1024])
    xs = din("xs", [128, 1024])
    cconv = din("cconv", [L, 2, 30, 256])
    ck = din("ck", [L, 2, 512, 512])
    cv = din("cv", [L, 2, 512, 512])
    wA = din("wA", [L * NSLAB_A, 128, 2048])
    wB = din("wB", [L * NSLAB_B, 128, 2816])
    gains_d = din("gains", [128, L * 3 * 8])
    convp_d = din("convp", [128, L * 2 * 34])
    tokp_d = din("tokp", [128, L * 640])
    gbias_d = din("gbias", [128, L * 2 * 128])
    gws_d = din("gws", [L * 4, 128, 128])
    chb_d = din("chb", [128, L * 8])
    biasT_d = din("biasT", [L, 128, 8 * 3 * 128])
    ident_d = din("ident", [128, 128])
    triu_d = din("triu", [128, 128])
    wAb = nc.dram_tensor("wAb", [L * NSLAB_A, 128, 2048], BF16, kind="Internal").ap()
    wBb = nc.dram_tensor("wBb", [L * NSLAB_B, 128, 2816], BF16, kind="Internal").ap()
    yp = dout("yp", [NTI * 512, 1024])
    ys = dout("ys", [128, 1024])
    pconv = dout("pconv", [L, 30, 256])
    pk = dout("pk", [L, 512, 512])
    pv = dout("pv", [L, 512, 512])
    sconv = dout("sconv", [L, 2, 30, 256])
    sk = dout("sk", [L, 2, 64, 512])
    sv = dout("sv", [L, 2, 64, 512])
    sgv = dout("sgv", [L, 2, 64, 256])

    es = contextlib.ExitStack()
    with es:
        def T(name, shape, dt):
            return es.enter_context(nc.sbuf_tensor(name, shape, dt))

        xres = T("xres", [128, NK, 512], F32)
        hT = T("hT", [128, NK, 512], BF16)
        mixT = hT
        sqb = hT
        aT = T("aT", [128, NF, 512], BF16)
        NSA, NSB = 4, 3
        ringA = T("ringA", [128, NSA, 2048], BF16)
        ringB = T("ringB", [128, NSB, 2816], BF16)
        kT = [T(f"kT{l}", [128, 4, 1024], BF16) for l in range(L)]
        Vt = [T(f"V{l}", [128, 8, 512], BF16) for l in range(L)]
        kTs = kT[0]
        Vs = Vt[0]
        Vs_new = T("Vs_new", [128, 2, 512], BF16)
        glu = [T(f"glu{l}", [128, 2, 542], F32) for l in range(L)]
        gluS = T("gluS", [128, 2, 2, 94], F32)
        cacc = T("cacc", [128, 2, 512], F32)
        gluB = T("gluB", [128, 2, 542], BF16)
        diag = T("diag", [128, 31, 128], BF16)
        qTz = T("qTz", [128, 2, 4, 512], BF16)
        uT = T("uT", [128, 2, 512], F32)
        zaT = T("zaT", [128, 2, 512], F32)
        csq = zaT
        vgb = T("vgb", [128, 4, 256], BF16)
        biasb = T("biasb", [128, 8 * 3 * 128], BF16)
        NR = 4
        sgb = T("sgb", [128, 2, 512], F32)
        st512 = T("st512", [128, NR, 512], F32)
        NTK = 4
        tk256 = T("tk256", [128, NTK, 256], F32)
        tkb = T("tkb", [128, 4, 256], BF16)
        small = T("small", [128, 8, 8], F32)
        bnst = T("bnst", [128, 4, 6], F32)
        PT = T("PT", [128, 3, 5, 128], BF16)
        xin = T("xin", [128, 1, 1024], F32)
        xout = T("xout", [128, 1, 1024], F32)
        cks = xin[:, 0, :].bitcast(BF16).rearrange("p (t f) -> p t f", f=512)
        identf = T("identf", [128, 128], F32)
        identb = T("identb", [128, 128], BF16)
        onesb = T("onesb", [128, 128], BF16)
        onesf = T("onesf", [128, 128], F32)
        epsT = T("epsT", [128, 1], F32)
        triuT = T("triuT", [128, 128], F32)
        gainsT = T("gainsT", [128, L * 3 * 8], F32)
        convpT = T("convpT", [128, L * 2 * 34], F32)
        tokpT = T("tokpT", [128, L * 640], F32)
        gbiasT = T("gbiasT", [128, L * 2 * 128], F32)
        chbT = T("chbT", [128, L * 8], F32)
        gwsin = T("gwsin", [128, 128], F32)
        WsT = T("WsT", [128, L * 4, 128], BF16)

        banks = [es.enter_context(nc.psum_tensor(f"bank{i}", [128, 512], F32)) for i in range(8)]
        Bbank = [Buf(f"bank{i}") for i in range(8)]
        bregs = [[Bbank[i]] for i in range(8)]

        def regs(b, c0, n):
            return [Bbank[b]]
        pools = {"a": [0, 1], "b": [2, 3], "c": [4, 5], "d": [6, 7], "w": [0, 1, 2, 3, 4, 5]}
        pctr = {k: 0 for k in pools}

        def nextbank(pool):
            b = pools[pool][pctr[pool] % len(pools[pool])]
            pctr[pool] += 1
            return b
        Bx = [Buf(f"xres{k}") for k in range(NK)]
        Bh = [Buf(f"hT{k}") for k in range(NK)]
        Bm = Bh
        Ba = [Buf(f"aT{j}") for j in range(NF)]
        BrA = [Buf(f"ringA{i}") for i in range(NSA)]
        BrB = [Buf(f"ringB{i}") for i in range(NSB)]
        BkT = [[Buf(f"kT{l}_{t}") for t in range(8)] for l in range(L)]
        BV = [[Buf(f"V{l}_{t}") for t in range(8)] for l in range(L)]
        BkTs = BkT[0]
        BVs = BV[0]
        BVn = [Buf("Vn0"), Buf("Vn1")]
        Bglu = [Buf(f"glu{l}") for l in range(L)]
        BgluS = [Buf(f"gluS{s}") for s in range(2)]
        Bcacc = [Buf("cacc0"), Buf("cacc1")]
        BgluB = [Buf("gluB0"), Buf("gluB1")]
        Bdiag = [Buf(f"diag{w}") for w in range(31)]
        BqT = [Buf(f"qT{g}") for g in range(4)]
        BuT = Buf("uT")
        Bza = Buf("zaT")
        Bcsq = Bza
        Bvgb = [Buf(f"vgb{g}") for g in range(4)]
        Bbias = Buf("biasb")
        Bsg = [Buf("sg0"), Buf("sg1")]
        Bst = [Buf(f"st{i}") for i in range(NR)]
        Btk = [Buf(f"tk{i}") for i in range(NTK)]
        Btkb = [Buf(f"tkb{i}") for i in range(4)]
        Bsmall = [Buf(f"small{i}") for i in range(8)]
        Bbn = [Buf(f"bn{i}") for i in range(4)]
        BPT = [Buf("PT0"), Buf("PT1"), Buf("PT2")]
        Bxin = [Buf("xin0")]
        Bxout = [Buf("xout0")]
        Bcks = Bxin[0]
        Bconst = Buf("const")
        Bgwsin = Buf("gwsin")
        BWs = Buf("WsT")
        rot = {}
        useq = [0]

        def nxt(name, n):
            i = rot.get(name, 0)
            rot[name] = i + 1
            return i % n

        semA = [S.dsem(f"semA{i}") for i in range(NSA)]
        semB = [S.dsem(f"semB{i}") for i in range(NSB)]
        sem_c = S.dsem("sem_c")
        sem_xin = [S.dsem("sem_xin0")]
        sem_out = [S.dsem("sem_out0")]
        dsems = {}

        def dsem_for(name):
            if name not in dsems:
                dsems[name] = S.dsem("ds_" + name)
            return dsems[name]
        out_tickets = []

        ntile = NTI + 1
        seqA = [l * NSLAB_A + i for _ in range(ntile) for l in range(L) for i in range(NSLAB_A)]
        seqB = [l * NSLAB_B + i for _ in range(ntile) for l in range(L) for i in range(NSLAB_B)]
        stA = {"iss": 0, "con": 0}
        stB = {"iss": 0, "con": 0}

        semWA = [S.dsem(f"semWA{i}") for i in range(NSA)]
        semWB = [S.dsem(f"semWB{i}") for i in range(NSB)]
        BdA = [Buf(f"wAb{i}") for i in range(L * NSLAB_A)]
        BdB = [Buf(f"wBb{i}") for i in range(L * NSLAB_B)]
        wb_on = not dbg.get("no_wb")

        def prefetchA():
            while stA["iss"] < stA["con"] + NSA and stA["iss"] < len(seqA):
                i = stA["iss"]
                s = i % NSA
                idx = seqA[i]
                if i < L * NSLAB_A or not wb_on:
                    S.dma("pool", dict(out=ringA[:, s, :], in_=wA[idx]), semA[s], writes=[BrA[s]])
                    if wb_on:
                        S.dma("sp", dict(out=wAb[idx], in_=ringA[:, s, :]), semWA[s], reads=[BrA[s]], writes=[BdA[idx]])
                else:
                    S.dma("pool", dict(out=ringA[:, s, :], in_=wAb[idx]), semA[s], reads=[BdA[idx]], writes=[BrA[s]])
                stA["iss"] += 1

        def prefetchB():
            while stB["iss"] < stB["con"] + NSB and stB["iss"] < len(seqB):
                i = stB["iss"]
                s = i % NSB
                idx = seqB[i]
                if i < L * NSLAB_B or not wb_on:
                    S.dma("pool", dict(out=ringB[:, s, :], in_=wB[idx]), semB[s], writes=[BrB[s]])
                    if wb_on:
                        S.dma("sp", dict(out=wBb[idx], in_=ringB[:, s, :]), semWB[s], reads=[BrB[s]], writes=[BdB[idx]])
                else:
                    S.dma("pool", dict(out=ringB[:, s, :], in_=wBb[idx]), semB[s], reads=[BdB[idx]], writes=[BrB[s]])
                stB["iss"] += 1

        def acquireA(expect, ahead=0):
            i = stA["con"] + ahead
            assert seqA[i] % NSLAB_A == expect % NSLAB_A and i < stA["iss"], (seqA[i], expect)
            s = i % NSA
            return ringA[:, s, :].rearrange("p (k c) -> p k c", c=256), BrA[s]

        def releaseA():
            stA["con"] += 1
            prefetchA()

        def acquireB():
            i = stB["con"]
            assert i < stB["iss"]
            s = i % NSB
            return ringB[:, s, :].rearrange("p (j c) -> p j c", c=128), BrB[s]

        def releaseB():
            stB["con"] += 1
            prefetchB()

        def mm(out, lhsT, rhs, start, stop, reads, writes, signal):
            S.op("pe", "matmul", dict(out=out, lhsT=lhsT, rhs=rhs, start=start, stop=stop),
                 reads=reads, writes=writes, signal=signal)

        def tr(out, in_, ident_ap, reads, writes, signal):
            S.op("pe", "transpose", dict(out=out, in_=in_, identity=ident_ap),
                 reads=reads, writes=writes, signal=signal)

        def act(out, in_, func, reads, writes, scale=None, bias=None):
            kw = dict(out=out, in_=in_, func=func)
            if scale is not None:
                kw["scale"] = scale
            if bias is not None:
                kw["bias"] = bias
            S.op("act", "activation", kw, reads=reads, writes=writes)

        def dve(meth, kw, reads, writes):
            S.op("dve", meth, kw, reads=reads, writes=writes)

        S.dma("sp", dict(out=identf[:], in_=ident_d[:, :]), sem_c, writes=[Bconst])
        S.dma("sp", dict(out=triuT[:], in_=triu_d[:, :]), sem_c, writes=[Bconst])
        S.dma("sp", dict(out=gainsT[:], in_=gains_d[:, :]), sem_c, writes=[Bconst])
        S.dma("sp", dict(out=convpT[:], in_=convp_d[:, :]), sem_c, writes=[Bconst])
        S.dma("sp", dict(out=tokpT[:], in_=tokp_d[:, :]), sem_c, writes=[Bconst])
        S.dma("sp", dict(out=gbiasT[:], in_=gbias_d[:, :]), sem_c, writes=[Bconst])
        S.dma("sp", dict(out=chbT[:], in_=chb_d[:, :]), sem_c, writes=[Bconst])
        prefetchA()
        prefetchB()
        Bc2 = Buf("const2")
        dve("tensor_copy", dict(out=identb[:], in_=identf[:]), [Bconst], [Bc2])
        dve("memset", dict(ap=onesb[:], constant=1.0), [], [Bc2])
        dve("memset", dict(ap=onesf[:], constant=1.0), [], [Bc2])
        dve("memset", dict(ap=epsT[:], constant=EPS), [], [Bc2])
        for l in range(L):
            dve("memset", dict(ap=glu[l][:, :, 0:30], constant=0.0), [], [Bglu[l]])
        CONST = [Bconst, Bc2]
        for lh in range(L * 4):
            S.dma("sp", dict(out=gwsin[:], in_=gws_d[lh]), dsem_for("gwsin"), writes=[Bgwsin])
            b = nextbank("d")
            tr(banks[b][:, 0:128], gwsin[:], identf[:], [Bgwsin] + CONST, regs(b, 0, 128), True)
            dve("tensor_tensor", dict(out=WsT[:, lh, :], in0=banks[b][:, 0:128], in1=triuT[:], op=ALU.mult),
                regs(b, 0, 128) + CONST, [BWs])

        def gain(l, i, kc):
            c = (l * 3 + i) * 8 + kc
            return gainsT[:, c:c + 1]

        def cpar(l, c, w):
            o = (l * 2 + c) * 34 + w
            return convpT[:, o:o + 1]

        def rmsnorm(l, gi, ncol):
            act(sqb[:, :, 0:ncol], xres[:, :, 0:ncol], AF.Square, Bx, Bh)
            b = nextbank("d")
            for kc in range(NK):
                mm(banks[b][:, 0:ncol], onesb[:], sqb[:, kc, 0:ncol], kc == 0, kc == NK - 1,
                   [Bh[kc]] + CONST, regs(b, 0, ncol), kc == NK - 1)
            i = nxt("st", NR)
            act(st512[:, i, 0:ncol], banks[b][:, 0:ncol], AF.Ln, regs(b, 0, ncol) + CONST, [Bst[i]],
                scale=1.0 / 1024.0, bias=epsT[:, 0:1])
            i2 = nxt("st", NR)
            act(st512[:, i2, 0:ncol], st512[:, i, 0:ncol], AF.Exp, [Bst[i]], [Bst[i2]], scale=-0.5)
            for kc in range(NK):
                dve("scalar_tensor_tensor",
                    dict(out=hT[:, kc, 0:ncol], in0=xres[:, kc, 0:ncol], scalar=gain(l, gi, kc), op0=ALU.mult,
                         in1=st512[:, i2, 0:ncol], op1=ALU.mult),
                    [Bx[kc], Bst[i2]] + CONST, [Bh[kc]])

        def ffn(l, w, ncol):
            rmsnorm(l, 0 if w == 0 else 2, ncol)
            base = 0 if w == 0 else 36
            for s in range(11):
                wg, bg = acquireA(base + 2 * s)
                wu, bu = acquireA(base + 2 * s + 1, 1)
                for jj in range(2):
                    j = 2 * s + jj
                    g_b = nextbank("a")
                    u_b = nextbank("b")
                    for kc in range(NK):
                        mm(banks[g_b][:, 0:ncol], wg[:, kc, jj * 128:(jj + 1) * 128], hT[:, kc, 0:ncol],
                           kc == 0, kc == NK - 1, [bg, Bh[kc]], regs(g_b, 0, ncol), kc == NK - 1)
                    for kc in range(NK):
                        mm(banks[u_b][:, 0:ncol], wu[:, kc, jj * 128:(jj + 1) * 128], hT[:, kc, 0:ncol],
                           kc == 0, kc == NK - 1, [bu, Bh[kc]], regs(u_b, 0, ncol), kc == NK - 1)
                    si = nxt("sg", 2)
                    act(sgb[:, si, 0:ncol], banks[g_b][:, 0:ncol], AF.Silu, regs(g_b, 0, ncol), [Bsg[si]])
                    dve("tensor_tensor", dict(out=aT[:, j, 0:ncol], in0=banks[u_b][:, 0:ncol],
                                              in1=sgb[:, si, 0:ncol], op=ALU.mult),
                        regs(u_b, 0, ncol) + [Bsg[si]], [Ba[j]])
                releaseA()
                releaseA()
            for m in range(NK):
                wd, bd = acquireB()
                y_b = nextbank("c")
                for j in range(NF):
                    mm(banks[y_b][:, 0:ncol], wd[:, j, :], aT[:, j, 0:ncol], j == 0, j == NF - 1,
                       [bd, Ba[j]], regs(y_b, 0, ncol), j == NF - 1)
                dve("scalar_tensor_tensor",
                    dict(out=xres[:, m, 0:ncol], in0=banks[y_b][:, 0:ncol], scalar=0.5, op0=ALU.mult,
                         in1=xres[:, m, 0:ncol], op1=ALU.add),
                    regs(y_b, 0, ncol) + [Bx[m]], [Bx[m]])
                releaseB()

        def gelu_from_psum(zb, zc0, n, npart, out_ap, out_bufs):
            z = banks[zb][0:npart, zc0:zc0 + n]
            zr = regs(zb, zc0, n)
            i = nxt("st", NR)
            act(st512[0:npart, i, 0:n], z, AF.Square, zr, [Bst[i]])
            dve("tensor_scalar", dict(out=st512[0:npart, i, 0:n], in0=st512[0:npart, i, 0:n], scalar1=0.044715,
                                      scalar2=1.0, op0=ALU.mult, op1=ALU.add), [Bst[i]], [Bst[i]])
            dve("tensor_tensor", dict(out=st512[0:npart, i, 0:n], in0=z, in1=st512[0:npart, i, 0:n], op=ALU.mult),
                zr + [Bst[i]], [Bst[i]])
            act(st512[0:npart, i, 0:n], st512[0:npart, i, 0:n], AF.Sigmoid, [Bst[i]], [Bst[i]], scale=GELU_C)
            dve("tensor_tensor", dict(out=out_ap, in0=z, in1=st512[0:npart, i, 0:n], op=ALU.mult),
                zr + [Bst[i]], out_bufs)

        def stage(k):
            if dbg.get("mixstop") == k:
                raise _Stop()

        def mixing(l, ncol, kind, tile_idx):
            sample = kind == "s"
            groups = [(0, 64), (64, 64)] if sample else [(g * 128, 128) for g in range(4)]
            last_prompt = (not sample) and tile_idx == NTI - 1
            first_prompt = (not sample) and tile_idx == 0
            rmsnorm(l, 1, ncol)
            S.dma("pool", dict(out=biasb[:], in_=biasT_d[l]), dsem_for("biasb"), writes=[Bbias])
            act(biasb[:], biasb[:], AF.Copy, [Bbias], [Bbias], scale=8.0)
            if not sample:
                if not first_prompt:
                    for t in range(4):
                        S.op("act", "activation", dict(out=kT[l][:, :, t * 128:(t + 1) * 128],
                                                       in_=kT[l][:, :, 512 + t * 128:512 + (t + 1) * 128], func=AF.Copy),
                             reads=[BkT[l][4 + t]], writes=[BkT[l][t]])
                        S.op("act", "activation", dict(out=Vt[l][:, t, :], in_=Vt[l][:, 4 + t, :], func=AF.Copy),
                             reads=[BV[l][4 + t]], writes=[BV[l][t]])
                    S.op("act", "activation", dict(out=glu[l][:, :, 0:30], in_=glu[l][:, :, 512:542], func=AF.Copy),
                         reads=[Bglu[l]], writes=[Bglu[l]])
            else:
                for s in range(2):
                    for c in range(2):
                        S.dma("sp", dict(out=gluS[:, s, c, 0:30],
                                         in_=cconv[l, s, :, c * 128:(c + 1) * 128].rearrange("t p -> p t"),
                                         allow_slow_non_contiguous=True), dsem_for(f"gluS{s}"), writes=[BgluS[s]])

            stage(1)
            w0, b0 = acquireA(22 + 0)
            for c in range(2):
                zb = nextbank("a")
                for kc in range(NK):
                    mm(banks[zb][:, 0:ncol], w0[:, kc, c * 128:(c + 1) * 128], hT[:, kc, 0:ncol],
                       kc == 0, kc == NK - 1, [b0, Bh[kc]], regs(zb, 0, ncol), kc == NK - 1)
                act(zaT[:, c, 0:ncol], banks[zb][:, 0:ncol], AF.Copy, regs(zb, 0, ncol), [Bza])
            releaseA()
            w1, b1 = acquireA(22 + 1)
            for c in range(2):
                zb = nextbank("b")
                for kc in range(NK):
                    mm(banks[zb][:, 0:ncol], w1[:, kc, c * 128:(c + 1) * 128], hT[:, kc, 0:ncol],
                       kc == 0, kc == NK - 1, [b1, Bh[kc]], regs(zb, 0, ncol), kc == NK - 1)
                si = nxt("sg", 2)
                act(sgb[:, si, 0:ncol], banks[zb][:, 0:ncol], AF.Sigmoid, regs(zb, 0, ncol), [Bsg[si]])
                if not sample:
                    dve("tensor_tensor", dict(out=glu[l][:, c, 30:542], in0=zaT[:, c, 0:512], in1=sgb[:, si, 0:512],
                                              op=ALU.mult), [Bza, Bsg[si]], [Bglu[l]])
                else:
                    for s in range(2):
                        dve("tensor_tensor", dict(out=gluS[:, s, c, 30:94], in0=zaT[:, c, s * 64:(s + 1) * 64],
                                                  in1=sgb[:, si, s * 64:(s + 1) * 64], op=ALU.mult),
                            [Bza, Bsg[si]], [BgluS[s]])
            releaseA()
            if last_prompt:
                for c in range(2):
                    out_tickets.append(S.dma("sp", dict(out=pconv[l, :, c * 128:(c + 1) * 128].rearrange("t p -> p t"),
                                                        in_=glu[l][:, c, 512:542], allow_slow_non_contiguous=True),
                                             dsem_for(f"st_glu{l}"), reads=[Bglu[l]]))
            if sample:
                for s in range(2):
                    for c in range(2):
                        out_tickets.append(S.dma("sp", dict(out=sconv[l, s, :, c * 128:(c + 1) * 128].rearrange("t p -> p t"),
                                                            in_=gluS[:, s, c, 64:94], allow_slow_non_contiguous=True),
                                                 dsem_for(f"st_gluS{s}"), reads=[BgluS[s]]))

            stage(2)
            deferred = []
            segs = [(0, 512, None)] if not sample else [(0, 64, 0), (64, 64, 1)]

            def conv_prep(seg, c):
                (c0, n, s) = seg
                src_all = glu[l][:, c, 0:30 + n] if s is None else gluS[:, s, c, 0:30 + n]
                sb = Bglu[l] if s is None else BgluS[s]
                act(gluB[:, c, 0:30 + n], src_all, AF.Copy, [sb], [BgluB[c]])
                for w in range(31):
                    if w % 2 == 0:
                        act(diag[:, w, :], identb[:], AF.Copy, CONST, [Bdiag[w]], scale=cpar(l, c, w))
                    else:
                        dve("tensor_scalar", dict(out=diag[:, w, :], in0=identb[:], scalar1=cpar(l, c, w),
                                                  scalar2=None, op0=ALU.mult), CONST, [Bdiag[w]])

            def conv_mm(seg, c, pool="c"):
                (c0, n, s) = seg
                yb = nextbank(pool)
                for w in range(31):
                    mm(banks[yb][:, 0:n], diag[:, w, :], gluB[:, c, w:w + n], w == 0, w == 30,
                       [Bdiag[w], BgluB[c]], [Bbank[yb]], w == 30)
                act(cacc[:, c, c0:c0 + n], banks[yb][:, 0:n], AF.Identity, [Bbank[yb]] + CONST, [Bcacc[c]],
                    bias=cpar(l, c, 31))

            def conv_ln(seg, p1="c", p2="d"):
                (c0, n, s) = seg
                act(csq[:, :, c0:c0 + n], cacc[:, :, c0:c0 + n], AF.Square, Bcacc, [Bcsq])
                b1_ = nextbank(p1)
                b2_ = nextbank(p2)
                for c in range(2):
                    mm(banks[b1_][:, 0:n], onesf[:], cacc[:, c, c0:c0 + n], c == 0, c == 1, [Bcacc[c]] + CONST,
                       regs(b1_, 0, n), c == 1)
                for c in range(2):
                    mm(banks[b2_][:, 0:n], onesf[:], csq[:, c, c0:c0 + n], c == 0, c == 1, [Bcsq] + CONST,
                       regs(b2_, 0, n), c == 1)
                im = nxt("st", NR)
                dve("tensor_scalar", dict(out=st512[:, im, 0:n], in0=banks[b1_][:, 0:n], scalar1=1.0 / 256.0,
                                          scalar2=None, op0=ALU.mult), regs(b1_, 0, n), [Bst[im]])
                iq = nxt("st", NR)
                dve("tensor_tensor", dict(out=st512[:, iq, 0:n], in0=st512[:, im, 0:n], in1=st512[:, im, 0:n],
                                          op=ALU.mult), [Bst[im]], [Bst[iq]])
                dve("scalar_tensor_tensor", dict(out=st512[:, iq, 0:n], in0=banks[b2_][:, 0:n], scalar=1.0 / 256.0,
                                                 op0=ALU.mult, in1=st512[:, iq, 0:n], op1=ALU.subtract),
                    regs(b2_, 0, n) + [Bst[iq]], [Bst[iq]])
                act(st512[:, iq, 0:n], st512[:, iq, 0:n], AF.Sqrt, [Bst[iq]] + CONST, [Bst[iq]], bias=epsT[:, 0:1])
                dve("reciprocal", dict(out=st512[:, iq, 0:n], in_=st512[:, iq, 0:n]), [Bst[iq]], [Bst[iq]])
                for c in range(2):
                    dve("tensor_tensor", dict(out=cacc[:, c, c0:c0 + n], in0=cacc[:, c, c0:c0 + n], in1=st512[:, im, 0:n],
                                              op=ALU.subtract), [Bcacc[c], Bst[im]], [Bcacc[c]])
                    dve("tensor_tensor", dict(out=cacc[:, c, c0:c0 + n], in0=cacc[:, c, c0:c0 + n], in1=st512[:, iq, 0:n],
                                              op=ALU.mult), [Bcacc[c], Bst[iq]], [Bcacc[c]])
                    deferred.append((c, c0, n))

            def slab2_mm():
                w2, b2 = acquireA(22 + 2)
                zbs = []
                for c in range(2):
                    zb = nextbank("a")
                    for kc in range(NK):
                        mm(banks[zb][:, 0:ncol], w2[:, kc, c * 128:(c + 1) * 128], hT[:, kc, 0:ncol],
                           kc == 0, kc == NK - 1, [b2, Bh[kc]], regs(zb, 0, ncol), kc == NK - 1)
                    zbs.append(zb)
                releaseA()
                return zbs

            def slab2_gelu(zbs):
                for c in range(2):
                    gelu_from_psum(zbs[c], 0, ncol, 128, uT[:, c, 0:ncol], [BuT])

            if sample:
                for seg in segs:
                    conv_prep(seg, 0)
                    conv_mm(seg, 0)
                    conv_prep(seg, 1)
                    conv_mm(seg, 1)
                    conv_ln(seg)
                stage(3)
                slab2_gelu(slab2_mm())
            else:
                seg = segs[0]
                conv_prep(seg, 0)
                stage(3)
                zbs2 = slab2_mm()
                conv_mm(seg, 0)
                conv_prep(seg, 1)
                slab2_gelu(zbs2)

            stage(4)
            tp = l * 640
            ng = len(groups)

            def tok_matmuls(ws_, bs_):
                zs = []
                for gi, (g0, n) in enumerate(groups):
                    zb = nextbank("w")
                    for kc in range(NK):
                        mm(banks[zb][0:n, 0:256], hT[:, kc, g0:g0 + n], ws_[:, kc, :], kc == 0, kc == NK - 1,
                           [bs_, Bh[kc]], [Bbank[zb]], kc == NK - 1)
                    zs.append(zb)
                return zs

            w3, b3 = acquireA(22 + 3)
            zs = tok_matmuls(w3, b3)
            releaseA()
            if not sample:
                conv_mm(segs[0], 1, "d")
                conv_ln(segs[0], "d", "d")
            G = list(enumerate(groups))

            def zz(gi):
                return banks[zs[gi]][0:groups[gi][1], 0:256]
            for gi, (g0, n) in G:
                act(tk256[0:n, gi, :], zz(gi), AF.Square, [Bbank[zs[gi]]], [Btk[gi]])
            for gi, (g0, n) in G:
                dve("tensor_scalar", dict(out=tk256[0:n, gi, :], in0=tk256[0:n, gi, :], scalar1=0.044715, scalar2=1.0,
                                          op0=ALU.mult, op1=ALU.add), [Btk[gi]], [Btk[gi]])
            for gi, (g0, n) in G:
                dve("tensor_tensor", dict(out=tk256[0:n, gi, :], in0=zz(gi), in1=tk256[0:n, gi, :], op=ALU.mult),
                    [Bbank[zs[gi]], Btk[gi]], [Btk[gi]])
            for gi, (g0, n) in G:
                act(tk256[0:n, gi, :], tk256[0:n, gi, :], AF.Sigmoid, [Btk[gi]], [Btk[gi]], scale=GELU_C)
            for gi, (g0, n) in G:
                dve("tensor_tensor", dict(out=tk256[0:n, gi, :], in0=zz(gi), in1=tk256[0:n, gi, :], op=ALU.mult),
                    [Bbank[zs[gi]], Btk[gi]], [Btk[gi]])
            for gi, (g0, n) in G:
                dve("bn_stats", dict(out=bnst[0:n, gi, :], in_=tk256[0:n, gi, :]), [Btk[gi]], [Bbn[gi]])
            for gi, (g0, n) in G:
                dve("bn_aggr", dict(out=small[0:n, gi, 0:2], in_=bnst[0:n, gi, :]), [Bbn[gi]], [Bsmall[gi]])
            for gi, (g0, n) in G:
                act(small[0:n, gi, 2:3], small[0:n, gi, 1:2], AF.Sqrt, [Bsmall[gi]] + CONST, [Bsmall[gi]],
                    bias=epsT[0:n, 0:1])
            for gi, (g0, n) in G:
                dve("reciprocal", dict(out=small[0:n, gi, 3:4], in_=small[0:n, gi, 2:3]), [Bsmall[gi]], [Bsmall[gi]])
            for gi, (g0, n) in G:
                dve("tensor_scalar", dict(out=tk256[0:n, gi, :], in0=tk256[0:n, gi, :], scalar1=small[0:n, gi, 0:1],
                                          scalar2=small[0:n, gi, 3:4], op0=ALU.subtract, op1=ALU.mult),
                    [Btk[gi], Bsmall[gi]], [Btk[gi]])
            for gi, (g0, n) in G:
                dve("tensor_tensor", dict(out=tk256[0:n, gi, :], in0=tk256[0:n, gi, :], in1=tokpT[0:n, tp:tp + 256],
                                          op=ALU.mult), [Btk[gi]] + CONST, [Btk[gi]])
            for gi, (g0, n) in G:
                dve("tensor_tensor", dict(out=tk256[0:n, gi, :], in0=tk256[0:n, gi, :],
                                          in1=tokpT[0:n, tp + 256:tp + 512], op=ALU.add), [Btk[gi]] + CONST, [Btk[gi]])
            for gi, (g0, n) in G:
                act(vgb[0:n, gi, :], tk256[0:n, gi, :], AF.Copy, [Btk[gi]], [Bvgb[gi]])
                if sample:
                    out_tickets.append(S.dma("sp", dict(out=sgv[l, gi], in_=tk256[0:n, gi, :]), dsem_for(f"st_tk{gi}"),
                                             reads=[Btk[gi]]))

            stage(5)
            pending = []

            def flush_pending():
                for f in pending:
                    f()
                del pending[:]
            for si_ in range(6):
                ws_, bs_ = acquireA(22 + 4 + si_)
                which = si_ // 2
                half_s = si_ % 2
                zs = tok_matmuls(ws_, bs_)
                releaseA()
                flush_pending()

                def zz(gi, zs=zs):
                    return banks[zs[gi]][0:groups[gi][1], 0:256]
                if which == 2:
                    for gi, (g0, n) in G:
                        zr = [Bbank[zs[gi]]]
                        if sample:
                            act(Vs_new[0:n, gi, half_s * 256:(half_s + 1) * 256], zz(gi), AF.Copy, zr, [BVn[gi]])
                        else:
                            act(Vt[l][:, 4 + gi, half_s * 256:(half_s + 1) * 256], zz(gi), AF.Copy, zr, [BV[l][4 + gi]])
                        if sample or last_prompt:
                            act(tk256[0:n, gi, :], zz(gi), AF.Copy, zr, [Btk[gi]])
                            dst = sv[l, gi, :, half_s * 256:(half_s + 1) * 256] if sample else \
                                pv[l, gi * 128:(gi + 1) * 128, half_s * 256:(half_s + 1) * 256]
                            out_tickets.append(S.dma("sp", dict(out=dst, in_=tk256[0:n, gi, :]), dsem_for(f"st_tk{gi}"),
                                                     reads=[Btk[gi]]))
                    continue
                for gi, (g0, n) in G:
                    act(tk256[0:n, gi, :], zz(gi), AF.Square, [Bbank[zs[gi]]], [Btk[gi]])
                for gi, (g0, n) in G:
                    dve("tensor_reduce", dict(out=small[0:n, gi, 0:4],
                                              in_=tk256[0:n, gi, :].rearrange("p (h d) -> p h d", d=64),
                                              axis=AX.X, op=ALU.add), [Btk[gi]], [Bsmall[gi]])
                for gi, (g0, n) in G:
                    act(small[0:n, gi, 0:4], small[0:n, gi, 0:4], AF.Sqrt, [Bsmall[gi]] + CONST, [Bsmall[gi]],
                        scale=1.0 / 64.0, bias=epsT[0:n, 0:1])
                for gi, (g0, n) in G:
                    dve("reciprocal", dict(out=small[0:n, gi, 4:8], in_=small[0:n, gi, 0:4]), [Bsmall[gi]], [Bsmall[gi]])
                for gi, (g0, n) in G:
                    dve("tensor_tensor", dict(out=tk256[0:n, gi, :].rearrange("p (h d) -> p h d", d=64),
                                              in0=zz(gi).rearrange("p (h d) -> p h d", d=64),
                                              in1=small[0:n, gi, 4:8].unsqueeze(2).to_broadcast([n, 4, 64]),
                                              op=ALU.mult), [Bbank[zs[gi]], Bsmall[gi]], [Btk[gi]])
                go = tp + 512 + which * 64
                for gi, (g0, n) in G:
                    dve("tensor_tensor", dict(out=tk256[0:n, gi, :].rearrange("p (h d) -> p h d", d=64),
                                              in0=tk256[0:n, gi, :].rearrange("p (h d) -> p h d", d=64),
                                              in1=tokpT[0:n, go:go + 64].unsqueeze(1).to_broadcast([n, 4, 64]),
                                              op=ALU.mult), [Btk[gi]] + CONST, [Btk[gi]])
                for gi, (g0, n) in G:
                    act(tkb[0:n, gi, :], tk256[0:n, gi, :], AF.Copy, [Btk[gi]], [Btkb[gi]])
                    if which == 1 and (sample or last_prompt):
                        dst = sk[l, gi, :, half_s * 256:(half_s + 1) * 256] if sample else \
                            pk[l, gi * 128:(gi + 1) * 128, half_s * 256:(half_s + 1) * 256]
                        out_tickets.append(S.dma("sp", dict(out=dst, in_=tk256[0:n, gi, :]), dsem_for(f"st_tk{gi}"),
                                                 reads=[Btk[gi]]))

                def transposes(which=which, half_s=half_s):
                    for gi, (g0, n) in G:
                        pb = nextbank("d")
                        pview = banks[pb][:, 0:128].bitcast(BF16)
                        for cc in range(2):
                            tr(pview[:, cc * 128:cc * 128 + n], tkb[0:n, gi, cc * 128:(cc + 1) * 128], identb[0:n, 0:n],
                               [Btkb[gi]] + CONST, [Bbank[pb]], cc == 1)
                        pv3 = pview.rearrange("p (c t) -> p c t", t=128)[:, :, 0:n]
                        ch0 = half_s * 2
                        if which == 0:
                            qb_ = [BqT[gi if not sample else 0]]
                            act(qTz[0:64, 0, ch0:ch0 + 2, g0:g0 + n], pv3[0:64], AF.Copy, [Bbank[pb]], qb_)
                            act(qTz[64:128, 1, ch0:ch0 + 2, g0:g0 + n], pv3[64:128], AF.Copy, [Bbank[pb]], qb_)
                        elif not sample:
                            act(kT[l][:, ch0:ch0 + 2, 512 + g0:512 + g0 + n], pv3, AF.Copy, [Bbank[pb]],
                                [BkT[l][4 + gi]])
                        else:
                            act(kTs_new[:, gi, ch0:ch0 + 2, 0:64], pv3, AF.Copy, [Bbank[pb]], [BkTn[gi]])
                if not dbg.get("no_tr"):
                    pending.append(transposes)
            flush_pending()

            stage(6)
            for (c, c0, n) in deferred:
                act(mixT[:, c, c0:c0 + n], cacc[:, c, c0:c0 + n], AF.Silu, [Bcacc[c]] + CONST, [Bm[c]],
                    scale=cpar(l, c, 32), bias=cpar(l, c, 33))

            for gi, (g0, n) in enumerate(groups):
                for c in range(2):
                    zb = nextbank("c")
                    for half in range(2):
                        h = 2 * c + half
                        mm(banks[zb][half * 64:(half + 1) * 64, 0:n], vgb[0:n, gi, h * 64:(h + 1) * 64],
                           WsT[0:n, l * 4 + h, 0:n], True, True, [Bvgb[gi], BWs], regs(zb, 0, n), half == 1)
                    i = nxt("st", NR)
                    go = (l * 2 + c) * 128
                    dve("tensor_tensor", dict(out=st512[:, i, 0:n], in0=banks[zb][:, 0:n], in1=gbiasT[:, go:go + n],
                                              op=ALU.add), regs(zb, 0, n) + CONST, [Bst[i]])
                    dve("tensor_tensor", dict(out=mixT[:, 2 + c, g0:g0 + n], in0=st512[:, i, 0:n],
                                              in1=uT[:, c, g0:g0 + n], op=ALU.mult), [Bst[i], BuT], [Bm[2 + c]])

            stage(7)
            MSLOT = {0: 0, 3: 1, 4: 2}
            CSLOT = {1: 0, 2: 1}

            def attn_unit_scores(u):
                par = u["seq"] % 3
                mb, cb = 2 * par, 2 * par + 1
                nq = u["nq"]
                keys = u["keys"]
                h = u["h"]
                for idx, (k_ap, v_ap, kb, vb, j) in enumerate(keys):
                    last = idx == len(keys) - 1
                    if j in MSLOT:
                        b, col = mb, MSLOT[j] * 128
                        bo = (h * 3 + MSLOT[j]) * 128
                        mm(banks[b][:, col:col + nq], k_ap, u["q"], True, False, [kb, u["qb"]], [Bbank[b]], False)
                        mm(banks[b][:, col:col + nq], identb[:], biasb[:, bo:bo + nq], False, True,
                           [Bbias] + CONST, [Bbank[b]], last)
                    else:
                        b, col = cb, CSLOT[j] * 128
                        mm(banks[b][:, col:col + nq], k_ap, u["q"], True, True, [kb, u["qb"]], [Bbank[b]], last)
                js = [k[4] for k in keys]
                m0 = min([MSLOT[j] for j in js if j in MSLOT])
                cs = [CSLOT[j] for j in js if j in CSLOT]
                act(PT[:, par, m0:3, 0:nq], banks[mb][:, 0:384].rearrange("p (t q) -> p t q", q=128)[:, m0:3, 0:nq],
                    AF.Exp, [Bbank[mb]], [BPT[par]], scale=0.125)
                if cs:
                    c0_ = min(cs)
                    act(PT[:, par, 3 + c0_:5, 0:nq],
                        banks[cb][:, 0:256].rearrange("p (t q) -> p t q", q=128)[:, c0_:2, 0:nq], AF.Exp,
                        [Bbank[cb]] + CONST, [BPT[par]], scale=0.125, bias=chbT[:, l * 8 + h:l * 8 + h + 1])

            def attn_unit_pv(u):
                par = u["seq"] % 3
                hb = (u["h"] % 2) * 64
                nq = u["nq"]
                keys = u["keys"]
                nkeys = len(keys)

                def slot(j):
                    return MSLOT[j] if j in MSLOT else 3 + CSLOT[j]
                for idx, (k_ap, v_ap, kb, vb, j) in enumerate(keys):
                    mm(banks[u["ob"]][hb:hb + 64, 0:nq], v_ap, PT[:, par, slot(j), 0:nq], idx == 0,
                       idx == nkeys - 1, [vb, BPT[par]], [Bbank[u["ob"]]], False)
                for idx, (k_ap, v_ap, kb, vb, j) in enumerate(keys):
                    mm(banks[u["db"]][hb:hb + 64, 128:128 + nq], onesb[:, 0:64], PT[:, par, slot(j), 0:nq], idx == 0,
                       idx == nkeys - 1, [BPT[par]] + CONST, [Bbank[u["db"]]], idx == nkeys - 1)
                if u["h"] % 2 == 1:
                    i = nxt("st", NR)
                    dve("reciprocal", dict(out=st512[:, i, 0:nq], in_=banks[u["db"]][:, 128:128 + nq]),
                        [Bbank[u["db"]]], [Bst[i]])
                    dve("tensor_tensor", dict(out=mixT[:, 4 + u["h"] // 2, u["c0"]:u["c0"] + nq],
                                              in0=banks[u["ob"]][:, 0:nq], in1=st512[:, i, 0:nq], op=ALU.mult),
                        [Bbank[u["ob"]], Bst[i]], [Bm[4 + u["h"] // 2]])

            def run_units(units):
                for i, u in enumerate(units):
                    u["seq"] = useq[0]
                    useq[0] += 1
                    attn_unit_scores(u)
                    if i >= 2:
                        attn_unit_pv(units[i - 2])
                for u in units[-2:]:
                    attn_unit_pv(u)

            if not sample and dbg.get("no_attn_p"):
                pass
            elif sample and dbg.get("no_attn_s"):
                pass
            elif not sample:
                units = []
                for p in range(4):
                    jmin = max(0, 4 - p) if first_prompt else dbg.get("jmin", 0)
                    for c in range(4):
                        ob = db = nextbank("d")
                        for half in range(2):
                            h = 2 * c + half
                            hb = half * 64
                            keys = []
                            for j in range(jmin, 5):
                                kt = p + j
                                keys.append((kT[l][:, c, kt * 128:(kt + 1) * 128],
                                             Vt[l][:, kt, h * 64:(h + 1) * 64], BkT[l][kt], BV[l][kt], j))
                            units.append(dict(h=h, nq=128, c0=p * 128, q=qTz[:, half, c, p * 128:(p + 1) * 128],
                                              qb=BqT[p], keys=keys, ob=ob, db=db))
                run_units(units)
            else:
                allprompt = [b_ for l_ in range(L) for b_ in BkT[l_]] + [b_ for l_ in range(L) for b_ in BV[l_]]
                for s in range(2):
                    S.dma("pool", dict(out=cks, in_=ck[l, s].rearrange("(t p) f -> p t f", p=128)),
                          dsem_for("cks"), writes=[Bcks])
                    S.dma("pool", dict(out=Vs[:, 0:4, :], in_=cv[l, s].rearrange("(t p) f -> p t f", p=128)),
                          dsem_for("Vs"), writes=BVs[0:4] + allprompt)
                    for t in range(4):
                        for c2 in range(2):
                            pb = nextbank("d")
                            pview = banks[pb][:, 0:128].bitcast(BF16)
                            for cc in range(2):
                                tr(pview[:, cc * 128:(cc + 1) * 128], cks[:, t, (c2 * 2 + cc) * 128:(c2 * 2 + cc + 1) * 128],
                                   identb[:], [Bcks] + CONST, regs(pb, 0, 128), cc == 1)
                            act(kTs[:, c2 * 2:c2 * 2 + 2, t * 128:(t + 1) * 128],
                                pview.rearrange("p (c t) -> p c t", t=128), AF.Copy, regs(pb, 0, 128),
                                [BkTs[t]] + allprompt)
                    units = []
                    for c in range(4):
                        ob = db = nextbank("d")
                        for half in range(2):
                            h = 2 * c + half
                            hb = half * 64
                            keys = []
                            for j in range(4):
                                keys.append((kTs[:, c, j * 128:(j + 1) * 128], Vs[:, j, h * 64:(h + 1) * 64],
                                             BkTs[j], BVs[j], j))
                            keys.append((kTs_new[:, s, c, :], Vs_new[:, s, h * 64:(h + 1) * 64],
                                         BkTn[s], BVn[s], 4))
                            units.append(dict(h=h, nq=64, c0=s * 64, q=qTz[:, half, c, s * 64:(s + 1) * 64],
                                              qb=BqT[0], keys=keys, ob=ob, db=db))
                    run_units(units)

            stage(8)
            for s4 in range(4):
                wo, bo_ = acquireA(22 + 10 + s4)
                for mm_ in range(2):
                    m = 2 * s4 + mm_
                    yb = nextbank("a" if mm_ == 0 else "b")
                    for kc in range(NK):
                        mm(banks[yb][:, 0:ncol], wo[:, kc, mm_ * 128:(mm_ + 1) * 128], mixT[:, kc, 0:ncol],
                           kc == 0, kc == NK - 1, [bo_, Bm[kc]], regs(yb, 0, ncol), kc == NK - 1)
                    dve("tensor_tensor", dict(out=xres[:, m, 0:ncol], in0=banks[yb][:, 0:ncol], in1=xres[:, m, 0:ncol],
                                              op=ALU.add), regs(yb, 0, ncol) + [Bx[m]], [Bx[m]])
                releaseA()

        kTs_new = T("kTs_new", [128, 2, 4, 128], BF16)
        BkTn = [Buf("kTn0"), Buf("kTn1")]
        dve("memset", dict(ap=kTs_new[:], constant=0.0), [], BkTn)
        dve("memset", dict(ap=qTz[:], constant=0.0), [], BqT)
        dve("memset", dict(ap=Vs_new[:], constant=0.0), [], BVn)

        def load_tile(kind, tile_idx):
            groups = [(0, 64), (64, 64)] if kind == "s" else [(g * 128, 128) for g in range(4)]
            for gi, (g0, n) in enumerate(groups):
                xi = nxt("xin", 1)
                src = xs[g0:g0 + n, :] if kind == "s" else xp[tile_idx * 512 + g0:tile_idx * 512 + g0 + n, :]
                S.dma("sp", dict(out=xin[0:n, xi, :], in_=src), sem_xin[xi], writes=[Bxin[xi]])
                for hh in range(2):
                    b = nextbank("c" if hh == 0 else "d")
                    for q4 in range(4):
                        kc = hh * 4 + q4
                        tr(banks[b][:, q4 * 128:q4 * 128 + n], xin[0:n, xi, kc * 128:(kc + 1) * 128], identf[0:n, 0:n],
                           [Bxin[xi]] + CONST, [Bbank[b]], q4 == 3)
                    src3 = banks[b][:, :].rearrange("p (c t) -> p c t", t=128)[:, :, 0:n]
                    if hh == 0:
                        act(xres[:, 0:4, g0:g0 + n], src3, AF.Copy, bregs[b], Bx[0:4])
                    else:
                        dve("tensor_copy", dict(out=xres[:, 4:8, g0:g0 + n], in_=src3), bregs[b], Bx[4:8])

        def store_tile(kind, tile_idx):
            groups = [(0, 64), (64, 64)] if kind == "s" else [(g * 128, 128) for g in range(4)]
            for gi, (g0, n) in enumerate(groups):
                xo = nxt("xout", 1)
                for hh in range(2):
                    b = nextbank("c" if hh == 0 else "d")
                    for q4 in range(4):
                        kc = hh * 4 + q4
                        tr(banks[b][0:n, q4 * 128:(q4 + 1) * 128], xres[:, kc, g0:g0 + n], identf[:, :],
                           [Bx[kc]] + CONST, [Bbank[b]], q4 == 3)
                    if hh == 0:
                        act(xout[0:n, xo, 0:512], banks[b][0:n, :], AF.Copy, bregs[b], [Bxout[xo]])
                    else:
                        dve("tensor_copy", dict(out=xout[0:n, xo, 512:1024], in_=banks[b][0:n, :]), bregs[b], [Bxout[xo]])
                dst = ys[g0:g0 + n, :] if kind == "s" else yp[tile_idx * 512 + g0:tile_idx * 512 + g0 + n, :]
                out_tickets.append(S.dma("sp", dict(out=dst, in_=xout[0:n, xo, :]), sem_out[xo], reads=[Bxout[xo]]))

        tiles = [("p", t) for t in range(NTI)] + [("s", 0)]
        for (kind, ti) in tiles:
            ncol = 128 if kind == "s" else 512
            load_tile(kind, ti)
            for l in range(dbg.get("layers", L)):
                if "ffn1" in dbg.get("phases", ("ffn1", "mix", "ffn2")):
                    ffn(l, 0, ncol)
                else:
                    stA["con"] += 22; stB["con"] += 8; prefetchA(); prefetchB()
                if "mix" in dbg.get("phases", ("ffn1", "mix", "ffn2")):
                    a0 = stA["con"]
                    try:
                        mixing(l, ncol, kind, ti)
                    except _Stop:
                        stA["con"] = a0 + 14
                        prefetchA()
                else:
                    stA["con"] += 14; prefetchA()
                if "ffn2" in dbg.get("phases", ("ffn1", "mix", "ffn2")):
                    ffn(l, 1, ncol)
                else:
                    stA["con"] += 22; stB["con"] += 8; prefetchA(); prefetchB()
            for l in range(dbg.get("layers", L), L):
                stA["con"] += NSLAB_A; stB["con"] += NSLAB_B; prefetchA(); prefetchB()
            store_tile(kind, ti)

        S.wait_all("sp", out_tickets)
        with nc.Block() as block:
            S.emit(block)
    return nc


def _slabA(w):
    C = w.shape[1]
    return np.ascontiguousarray(w.reshape(8, 128, C // 256, 256).transpose(2, 1, 0, 3)).reshape(C // 256, 128, 2048)


def _slabB(w):
    return np.ascontiguousarray(w.reshape(22, 128, 8, 128).transpose(2, 1, 0, 3)).reshape(8, 128, 2816)


def prepare_shared(inp):
    f = np.float32
    wA = np.empty((L * NSLAB_A, 128, 2048), f)
    wB = np.empty((L * NSLAB_B, 128, 2816), f)
    for l in range(L):
        o = l * NSLAB_A
        for w, base in ((0, 0), (1, 36)):
            g = _slabA(np.asarray(inp[f"ffn{w + 1}_w_gate"][l]))
            u = _slabA(np.asarray(inp[f"ffn{w + 1}_w_up"][l]))
            for s in range(11):
                wA[o + base + 2 * s] = g[s]
                wA[o + base + 2 * s + 1] = u[s]
            wB[l * NSLAB_B + w * 8:l * NSLAB_B + w * 8 + 8] = _slabB(np.asarray(inp[f"ffn{w + 1}_w_down"][l]))
        wA[o + 22:o + 32] = _slabA(np.asarray(inp["w_in"][l]))
        wA[o + 32:o + 36] = _slabA(np.asarray(inp["w_out"][l]))
    gains = np.empty((128, L, 3, 8), f)
    for l in range(L):
        for i, nm in enumerate(("ffn1_norm", "mix_norm", "ffn2_norm")):
            gains[:, l, i, :] = np.asarray(inp[nm][l]).reshape(8, 128).T
    convp = np.empty((128, L, 2, 34), f)
    for l in range(L):
        cw = np.asarray(inp["conv_w"][l])
        for c in range(2):
            convp[:, l, c, 0:31] = cw[:, c * 128:(c + 1) * 128].T
            convp[:, l, c, 31] = np.asarray(inp["conv_b"][l])[c * 128:(c + 1) * 128]
            convp[:, l, c, 32] = np.asarray(inp["conv_ln_g"][l])[c * 128:(c + 1) * 128]
            convp[:, l, c, 33] = np.asarray(inp["conv_ln_b"][l])[c * 128:(c + 1) * 128]
    tokp = np.empty((128, L, 640), f)
    for l in range(L):
        tokp[:, l, 0:256] = np.asarray(inp["gmlp_ln_g"][l])[None, :]
        tokp[:, l, 256:512] = np.asarray(inp["gmlp_ln_b"][l])[None, :]
        tokp[:, l, 512:576] = np.asarray(inp["q_norm"][l])[None, :]
        tokp[:, l, 576:640] = np.asarray(inp["k_norm"][l])[None, :]
    gb = np.asarray(inp["gmlp_b"])
    gbias = np.empty((128, L, 2, 128), f)
    for l in range(L):
        for c in range(2):
            gbias[0:64, l, c, :] = gb[l, 2 * c][None, :]
            gbias[64:128, l, c, :] = gb[l, 2 * c + 1][None, :]
    rb = np.asarray(inp["rel_bias"])
    chb = np.empty((128, L, 8), f)
    chb[:] = rb[None, :, :, 256]
    kl = np.arange(128)[:, None]
    ql = np.arange(128)[None, :]
    biasT = np.empty((L, 128, 8, 3, 128), f)
    for ji, j in enumerate((0, 3, 4)):
        rel = ql - kl + (4 - j) * 128
        idx = np.clip(rel, -128, 128) + 128
        tab = rb[:, :, idx]
        if j == 4:
            msk = (kl >= 64) & (ql < 64)
        elif j == 0:
            msk = (kl < 64) & (ql >= 64)
        else:
            msk = np.zeros((128, 128), bool)
        tab = np.where(msk[None, None], f(-30000.0), tab)
        biasT[:, :, :, ji, :] = tab.transpose(0, 2, 1, 3)
    return dict(
        wA=wA, wB=wB, gains=gains.reshape(128, -1), convp=convp.reshape(128, -1), tokp=tokp.reshape(128, -1),
        gbias=gbias.reshape(128, -1), gws=np.ascontiguousarray(np.asarray(inp["gmlp_ws"], f).reshape(L * 4, 128, 128)),
        chb=chb.reshape(128, -1), biasT=biasT.reshape(L, 128, -1), ident=np.eye(128, dtype=f),
        triu=np.triu(np.ones((128, 128), f)),
    )


def run(inp, NTI):
    SEQ = 512 * (2 * NTI - 2)
    x_prompt = np.asarray(inp["x_prompt"], np.float32)
    x_sample = np.asarray(inp["x_sample"], np.float32)
    B = x_prompt.shape[0]
    assert x_prompt.shape[1] == SEQ and B * 2 == 8 and x_sample.shape[0] == 16
    shared = prepare_shared(inp)
    cconv = np.asarray(inp["cache_conv"], np.float32)
    ck = np.asarray(inp["cache_k"], np.float32).reshape(L, 16, 512, 512)
    cv = np.asarray(inp["cache_v"], np.float32).reshape(L, 16, 512, 512)
    in_maps = []
    for c in range(8):
        b, half = c // 2, c % 2
        t0 = 0 if half == 0 else SEQ - NTI * 512
        m = dict(shared)
        m["xp"] = np.ascontiguousarray(x_prompt[b, t0:t0 + NTI * 512])
        m["xs"] = np.ascontiguousarray(x_sample[2 * c:2 * c + 2].reshape(128, 1024))
        m["cconv"] = np.ascontiguousarray(cconv[:, 2 * c:2 * c + 2])
        m["ck"] = np.ascontiguousarray(ck[:, 2 * c:2 * c + 2])
        m["cv"] = np.ascontiguousarray(cv[:, 2 * c:2 * c + 2])
        in_maps.append(m)
    nc = build_program(NTI)
    res = run_bass_kernel_spmd(nc, in_maps, core_ids=list(range(8)))
    R = res.results
    f = np.float32
    y_prompt = np.empty((B, SEQ, 1024), f)
    y_sample = np.empty((16, 64, 1024), f)
    p_conv = np.empty((L, B, 30, 256), f)
    p_k = np.empty((L, B, 512, 8, 64), f)
    p_v = np.empty((L, B, 512, 8, 64), f)
    s_conv = np.empty((L, 16, 30, 256), f)
    s_k = np.empty((L, 16, 64, 8, 64), f)
    s_v = np.empty((L, 16, 64, 8, 64), f)
    s_gv = np.empty((L, 16, 64, 4, 64), f)
    for c in range(8):
        b, half = c // 2, c % 2
        r = R[c]
        if half == 0:
            y_prompt[b, 0:NTI * 512] = r["yp"]
        else:
            y_prompt[b, NTI * 512:SEQ] = r["yp"][1024:]
            p_conv[:, b] = r["pconv"]
            p_k[:, b] = r["pk"].reshape(L, 512, 8, 64)
            p_v[:, b] = r["pv"].reshape(L, 512, 8, 64)
        y_sample[2 * c:2 * c + 2] = r["ys"].reshape(2, 64, 1024)
        s_conv[:, 2 * c:2 * c + 2] = r["sconv"]
        s_k[:, 2 * c:2 * c + 2] = r["sk"].reshape(L, 2, 64, 8, 64)
        s_v[:, 2 * c:2 * c + 2] = r["sv"].reshape(L, 2, 64, 8, 64)
        s_gv[:, 2 * c:2 * c + 2] = r["sgv"].reshape(L, 2, 64, 4, 64)
    return (y_prompt, y_sample, p_conv, p_k, p_v, s_conv, s_k, s_v, s_gv)


def kernel(**inputs):
    return run(inputs, NTI_FULL)
```

```python
import contextlib
import numpy as np
import concourse.bass as bass
import concourse.mybir as mybir
from concourse.bass_utils import run_bass_kernel_spmd

F32 = mybir.dt.float32
BF16 = mybir.dt.bfloat16
AF = mybir.ActivationFunctionType
ALU = mybir.AluOpType
AX = mybir.AxisListType

L = 2
NK = 8
NF = 22
NSLAB_A = 58
NSLAB_B = 16
EPS = 1e-6
NTI_FULL = 9
GELU_C = 1.5957691216057308


class Buf:
    __slots__ = ("name", "last_w", "readers")

    def __init__(self, name):
        self.name = name
        self.last_w = None
        self.readers = []


class DmaSem:
    def __init__(self, handle, key):
        self.handle = handle
        self.key = key
        self.count = 0


class Sched:
    ENG = ("pe", "act", "dve", "pool", "sp")

    def __init__(self, nc):
        self.nc = nc
        self.items = {e: [] for e in self.ENG}
        self.cnt = {e: 0 for e in self.ENG}
        self.seen = {e: {} for e in self.ENG}
        self.semh = {}
        for e in self.ENG:
            self.semh["p_" + e] = nc.alloc_semaphore(name="p_" + e)

    def dsem(self, name):
        h = self.nc.alloc_semaphore(name=name)
        self.semh[name] = h
        return DmaSem(h, name)

    def _waits(self, eng, reads, writes):
        w = {}

        def add(t, raw):
            if t is None:
                return
            k, v = t
            if k == "p_" + eng and not raw:
                return
            if w.get(k, 0) < v:
                w[k] = v
        for b in reads:
            add(b.last_w, True)
        for b in writes:
            add(b.last_w, False)
            for t in b.readers:
                add(t, False)
        need = []
        seen = self.seen[eng]
        for k, v in w.items():
            if seen.get(k, 0) < v:
                seen[k] = v
                need.append((k, v))
        return need

    def op(self, eng, meth, kw, reads=(), writes=(), signal=True):
        need = self._waits(eng, reads, writes)
        if signal:
            self.cnt[eng] += 1
            t = ("p_" + eng, self.cnt[eng])
        else:
            t = ("p_" + eng, self.cnt[eng] + 1)
        self.items[eng].append((need, (meth, kw), ("p_" + eng, 1) if signal else None))
        for b in reads:
            if len(b.readers) > 64:
                b.readers = _compress(b.readers)
            b.readers.append(t)
        for b in writes:
            b.last_w = t
            b.readers = []
        return t

    def dma(self, eng, kw, sem, reads=(), writes=()):
        need = self._waits(eng, reads, writes)
        sem.count += 16
        t = (sem.key, sem.count)
        self.items[eng].append((need, ("dma_start", kw), (sem.key, 16)))
        for b in reads:
            b.readers.append(t)
        for b in writes:
            b.last_w = t
            b.readers = []
        return t

    def wait_all(self, eng, tickets):
        need = []
        seen = self.seen[eng]
        for (k, v) in _compress(tickets):
            if seen.get(k, 0) < v:
                seen[k] = v
                need.append((k, v))
        self.items[eng].append((need, None, None))

    def emit(self, block):
        def mk(e):
            items = self.items[e]

            def body(engh):
                for need, fn, sig in items:
                    for k, v in need:
                        engh.wait_ge(self.semh[k], v)
                    if fn is not None:
                        ins = getattr(engh, fn[0])(**fn[1])
                        if sig is not None:
                            ins.then_inc(self.semh[sig[0]], sig[1])
            return body
        block.tensor(mk("pe"))
        block.scalar(mk("act"))
        block.vector(mk("dve"))
        block.gpsimd(mk("pool"))
        block.sync(mk("sp"))


class _Stop(Exception):
    pass


def _compress(ts):
    m = {}
    for k, v in ts:
        if m.get(k, 0) < v:
            m[k] = v
    return list(m.items())


def build_program(NTI, dbg=None):
    nc = bass.Bass("TRN2", target_bir_lowering=False)
    S = Sched(nc)
    dbg = dbg or {}

    def din(name, shape):
        return nc.dram_tensor(name, shape, F32, kind="ExternalInput").ap()

    def dout(name, shape):
        return nc.dram_tensor(name, shape, F32, kind="ExternalOutput").ap()

    xp = din("xp", [NTI * 512, 1024])
    xs = din("xs", [128, 1024])
    cconv = din("cconv", [L, 2, 30, 256])
    ck = din("ck", [L, 2, 512, 512])
    cv = din("cv", [L, 2, 512, 512])
    wA = din("wA", [L * NSLAB_A, 128, 2048])
    wB = din("wB", [L * NSLAB_B, 128, 2816])
    gains_d = din("gains", [128, L * 3 * 8])
    convp_d = din("convp", [128, L * 2 * 34])
    tokp_d = din("tokp", [128, L * 640])
    gbias_d = din("gbias", [128, L * 2 * 128])
    gws_d = din("gws", [L * 4, 128, 128])
    chb_d = din("chb", [128, L * 8])
    biasT_d = din("biasT", [L, 128, 8 * 3 * 128])
    ident_d = din("ident", [128, 128])
    triu_d = din("triu", [128, 128])
    wAb = nc.dram_tensor("wAb", [L * NSLAB_A, 128, 2048], BF16, kind="Internal").ap()
    wBb = nc.dram_tensor("wBb", [L * NSLAB_B, 128, 2816], BF16, kind="Internal").ap()
    yp = dout("yp", [NTI * 512, 1024])
    ys = dout("ys", [128, 1024])
    pconv = dout("pconv", [L, 30, 256])
    pk = dout("pk", [L, 512, 512])
    pv = dout("pv", [L, 512, 512])
    sconv = dout("sconv", [L, 2, 30, 256])
    sk = dout("sk", [L, 2, 64, 512])
    sv = dout("sv", [L, 2, 64, 512])
    sgv = dout("sgv", [L, 2, 64, 256])

    es = contextlib.ExitStack()
    with es:
        def T(name, shape, dt):
            return es.enter_context(nc.sbuf_tensor(name, shape, dt))

        xres = T("xres", [128, NK, 512], F32)
        hT = T("hT", [128, NK, 512], BF16)
        mixT = hT
        sqb = hT
        aT = T("aT", [128, NF, 512], BF16)
        NSA, NSB = 4, 3
        ringA = T("ringA", [128, NSA, 2048], BF16)
        ringB = T("ringB", [128, NSB, 2816], BF16)
        kT = [T(f"kT{l}", [128, 4, 1024], BF16) for l in range(L)]
        Vt = [T(f"V{l}", [128, 8, 512], BF16) for l in range(L)]
        kTs = kT[0]
        Vs = Vt[0]
        Vs_new = T("Vs_new", [128, 2, 512], BF16)
        glu = [T(f"glu{l}", [128, 2, 542], F32) for l in range(L)]
        gluS = T("gluS", [128, 2, 2, 94], F32)
        cacc = T("cacc", [128, 2, 512], F32)
        gluB = T("gluB", [128, 2, 542], BF16)
        diag = T("diag", [128, 31, 128], BF16)
        qTz = T("qTz", [128, 2, 4, 512], BF16)
        uT = T("uT", [128, 2, 512], F32)
        zaT = T("zaT", [128, 2, 512], F32)
        csq = zaT
        vgb = T("vgb", [128, 4, 256], BF16)
        biasb = T("biasb", [128, 8 * 3 * 128], BF16)
        NR = 4
        sgb = T("sgb", [128, 2, 512], F32)
        st512 = T("st512", [128, NR, 512], F32)
        NTK = 4
        tk256 = T("tk256", [128, NTK, 256], F32)
        tkb = T("tkb", [128, 4, 256], BF16)
        small = T("small", [128, 8, 8], F32)
        bnst = T("bnst", [128, 4, 6], F32)
        PT = T("PT", [128, 3, 5, 128], BF16)
        xin = T("xin", [128, 1, 1024], F32)
        xout = T("xout", [128, 1, 1024], F32)
        cks = xin[:, 0, :].bitcast(BF16).rearrange("p (t f) -> p t f", f=512)
        identf = T("identf", [128, 128], F32)
        identb = T("identb", [128, 128], BF16)
        onesb = T("onesb", [128, 128], BF16)
        onesf = T("onesf", [128, 128], F32)
        epsT = T("epsT", [128, 1], F32)
        triuT = T("triuT", [128, 128], F32)
        gainsT = T("gainsT", [128, L * 3 * 8], F32)
        convpT = T("convpT", [128, L * 2 * 34], F32)
        tokpT = T("tokpT", [128, L * 640], F32)
        gbiasT = T("gbiasT", [128, L * 2 * 128], F32)
        chbT = T("chbT", [128, L * 8], F32)
        gwsin = T("gwsin", [128, 128], F32)
        WsT = T("WsT", [128, L * 4, 128], BF16)

        banks = [es.enter_context(nc.psum_tensor(f"bank{i}", [128, 512], F32)) for i in range(8)]
        Bbank = [Buf(f"bank{i}") for i in range(8)]
        bregs = [[Bbank[i]] for i in range(8)]

        def regs(b, c0, n):
            return [Bbank[b]]
        pools = {"a": [0, 1], "b": [2, 3], "c": [4, 5], "d": [6, 7], "w": [0, 1, 2, 3, 4, 5]}
        pctr = {k: 0 for k in pools}

        def nextbank(pool):
            b = pools[pool][pctr[pool] % len(pools[pool])]
            pctr[pool] += 1
            return b
        Bx = [Buf(f"xres{k}") for k in range(NK)]
        Bh = [Buf(f"hT{k}") for k in range(NK)]
        Bm = Bh
        Ba = [Buf(f"aT{j}") for j in range(NF)]
        BrA = [Buf(f"ringA{i}") for i in range(NSA)]
        BrB = [Buf(f"ringB{i}") for i in range(NSB)]
        BkT = [[Buf(f"kT{l}_{t}") for t in range(8)] for l in range(L)]
        BV = [[Buf(f"V{l}_{t}") for t in range(8)] for l in range(L)]
        BkTs = BkT[0]
        BVs = BV[0]
        BVn = [Buf("Vn0"), Buf("Vn1")]
        Bglu = [Buf(f"glu{l}") for l in range(L)]
        BgluS = [Buf(f"gluS{s}") for s in range(2)]
        Bcacc = [Buf("cacc0"), Buf("cacc1")]
        BgluB = [Buf("gluB0"), Buf("gluB1")]
        Bdiag = [Buf(f"diag{w}") for w in range(31)]
        BqT = [Buf(f"qT{g}") for g in range(4)]
        BuT = Buf("uT")
        Bza = Buf("zaT")
        Bcsq = Bza
        Bvgb = [Buf(f"vgb{g}") for g in range(4)]
        Bbias = Buf("biasb")
        Bsg = [Buf("sg0"), Buf("sg1")]
        Bst = [Buf(f"st{i}") for i in range(NR)]
        Btk = [Buf(f"tk{i}") for i in range(NTK)]
        Btkb = [Buf(f"tkb{i}") for i in range(4)]
        Bsmall = [Buf(f"small{i}") for i in range(8)]
        Bbn = [Buf(f"bn{i}") for i in range(4)]
        BPT = [Buf("PT0"), Buf("PT1"), Buf("PT2")]
        Bxin = [Buf("xin0")]
        Bxout = [Buf("xout0")]
        Bcks = Bxin[0]
        Bconst = Buf("const")
        Bgwsin = Buf("gwsin")
        BWs = Buf("WsT")
        rot = {}
        useq = [0]

        def nxt(name, n):
            i = rot.get(name, 0)
            rot[name] = i + 1
            return i % n

        semA = [S.dsem(f"semA{i}") for i in range(NSA)]
        semB = [S.dsem(f"semB{i}") for i in range(NSB)]
        sem_c = S.dsem("sem_c")
        sem_xin = [S.dsem("sem_xin0")]
        sem_out = [S.dsem("sem_out0")]
        dsems = {}

        def dsem_for(name):
            if name not in dsems:
                dsems[name] = S.dsem("ds_" + name)
            return dsems[name]
        out_tickets = []

        ntile = NTI + 1
        seqA = [l * NSLAB_A + i for _ in range(ntile) for l in range(L) for i in range(NSLAB_A)]
        seqB = [l * NSLAB_B + i for _ in range(ntile) for l in range(L) for i in range(NSLAB_B)]
        stA = {"iss": 0, "con": 0}
        stB = {"iss": 0, "con": 0}

        semWA = [S.dsem(f"semWA{i}") for i in range(NSA)]
        semWB = [S.dsem(f"semWB{i}") for i in range(NSB)]
        BdA = [Buf(f"wAb{i}") for i in range(L * NSLAB_A)]
        BdB = [Buf(f"wBb{i}") for i in range(L * NSLAB_B)]
        wb_on = not dbg.get("no_wb")

        def prefetchA():
            while stA["iss"] < stA["con"] + NSA and stA["iss"] < len(seqA):
                i = stA["iss"]
                s = i % NSA
                idx = seqA[i]
                if i < L * NSLAB_A or not wb_on:
                    S.dma("pool", dict(out=ringA[:, s, :], in_=wA[idx]), semA[s], writes=[BrA[s]])
                    if wb_on:
                        S.dma("sp", dict(out=wAb[idx], in_=ringA[:, s, :]), semWA[s], reads=[BrA[s]], writes=[BdA[idx]])
                else:
                    S.dma("pool", dict(out=ringA[:, s, :], in_=wAb[idx]), semA[s], reads=[BdA[idx]], writes=[BrA[s]])
                stA["iss"] += 1

        def prefetchB():
            while stB["iss"] < stB["con"] + NSB and stB["iss"] < len(seqB):
                i = stB["iss"]
                s = i % NSB
                idx = seqB[i]
                if i < L * NSLAB_B or not wb_on:
                    S.dma("pool", dict(out=ringB[:, s, :], in_=wB[idx]), semB[s], writes=[BrB[s]])
                    if wb_on:
                        S.dma("sp", dict(out=wBb[idx], in_=ringB[:, s, :]), semWB[s], reads=[BrB[s]], writes=[BdB[idx]])
                else:
                    S.dma("pool", dict(out=ringB[:, s, :], in_=wBb[idx]), semB[s], reads=[BdB[idx]], writes=[BrB[s]])
                stB["iss"] += 1

        def acquireA(expect, ahead=0):
            i = stA["con"] + ahead
            assert seqA[i] % NSLAB_A == expect % NSLAB_A and i < stA["iss"], (seqA[i], expect)
            s = i % NSA
            return ringA[:, s, :].rearrange("p (k c) -> p k c", c=256), BrA[s]

        def releaseA():
            stA["con"] += 1
            prefetchA()

        def acquireB():
            i = stB["con"]
            assert i < stB["iss"]
            s = i % NSB
            return ringB[:, s, :].rearrange("p (j c) -> p j c", c=128), BrB[s]

        def releaseB():
            stB["con"] += 1
            prefetchB()

        def mm(out, lhsT, rhs, start, stop, reads, writes, signal):
            S.op("pe", "matmul", dict(out=out, lhsT=lhsT, rhs=rhs, start=start, stop=stop),
                 reads=reads, writes=writes, signal=signal)

        def tr(out, in_, ident_ap, reads, writes, signal):
            S.op("pe", "transpose", dict(out=out, in_=in_, identity=ident_ap),
                 reads=reads, writes=writes, signal=signal)

        def act(out, in_, func, reads, writes, scale=None, bias=None):
            kw = dict(out=out, in_=in_, func=func)
            if scale is not None:
                kw["scale"] = scale
            if bias is not None:
                kw["bias"] = bias
            S.op("act", "activation", kw, reads=reads, writes=writes)

        def dve(meth, kw, reads, writes):
            S.op("dve", meth, kw, reads=reads, writes=writes)

        S.dma("sp", dict(out=identf[:], in_=ident_d[:, :]), sem_c, writes=[Bconst])
        S.dma("sp", dict(out=triuT[:], in_=triu_d[:, :]), sem_c, writes=[Bconst])
        S.dma("sp", dict(out=gainsT[:], in_=gains_d[:, :]), sem_c, writes=[Bconst])
        S.dma("sp", dict(out=convpT[:], in_=convp_d[:, :]), sem_c, writes=[Bconst])
        S.dma("sp", dict(out=tokpT[:], in_=tokp_d[:, :]), sem_c, writes=[Bconst])
        S.dma("sp", dict(out=gbiasT[:], in_=gbias_d[:, :]), sem_c, writes=[Bconst])
        S.dma("sp", dict(out=chbT[:], in_=chb_d[:, :]), sem_c, writes=[Bconst])
        prefetchA()
        prefetchB()
        Bc2 = Buf("const2")
        dve("tensor_copy", dict(out=identb[:], in_=identf[:]), [Bconst], [Bc2])
        dve("memset", dict(ap=onesb[:], constant=1.0), [], [Bc2])
        dve("memset", dict(ap=onesf[:], constant=1.0), [], [Bc2])
        dve("memset", dict(ap=epsT[:], constant=EPS), [], [Bc2])
        for l in range(L):
            dve("memset", dict(ap=glu[l][:, :, 0:30], constant=0.0), [], [Bglu[l]])
        CONST = [Bconst, Bc2]
        for lh in range(L * 4):
            S.dma("sp", dict(out=gwsin[:], in_=gws_d[lh]), dsem_for("gwsin"), writes=[Bgwsin])
            b = nextbank("d")
            tr(banks[b][:, 0:128], gwsin[:], identf[:], [Bgwsin] + CONST, regs(b, 0, 128), True)
            dve("tensor_tensor", dict(out=WsT[:, lh, :], in0=banks[b][:, 0:128], in1=triuT[:], op=ALU.mult),
                regs(b, 0, 128) + CONST, [BWs])

        def gain(l, i, kc):
            c = (l * 3 + i) * 8 + kc
            return gainsT[:, c:c + 1]

        def cpar(l, c, w):
            o = (l * 2 + c) * 34 + w
            return convpT[:, o:o + 1]

        def rmsnorm(l, gi, ncol):
            act(sqb[:, :, 0:ncol], xres[:, :, 0:ncol], AF.Square, Bx, Bh)
            b = nextbank("d")
            for kc in range(NK):
                mm(banks[b][:, 0:ncol], onesb[:], sqb[:, kc, 0:ncol], kc == 0, kc == NK - 1,
                   [Bh[kc]] + CONST, regs(b, 0, ncol), kc == NK - 1)
            i = nxt("st", NR)
            act(st512[:, i, 0:ncol], banks[b][:, 0:ncol], AF.Ln, regs(b, 0, ncol) + CONST, [Bst[i]],
                scale=1.0 / 1024.0, bias=epsT[:, 0:1])
            i2 = nxt("st", NR)
            act(st512[:, i2, 0:ncol], st512[:, i, 0:ncol], AF.Exp, [Bst[i]], [Bst[i2]], scale=-0.5)
            for kc in range(NK):
                dve("scalar_tensor_tensor",
                    dict(out=hT[:, kc, 0:ncol], in0=xres[:, kc, 0:ncol], scalar=gain(l, gi, kc), op0=ALU.mult,
                         in1=st512[:, i2, 0:ncol], op1=ALU.mult),
                    [Bx[kc], Bst[i2]] + CONST, [Bh[kc]])

        def ffn(l, w, ncol):
            rmsnorm(l, 0 if w == 0 else 2, ncol)
            base = 0 if w == 0 else 36
            for s in range(11):
                wg, bg = acquireA(base + 2 * s)
                wu, bu = acquireA(base + 2 * s + 1, 1)
                for jj in range(2):
                    j = 2 * s + jj
                    g_b = nextbank("a")
                    u_b = nextbank("b")
                    for kc in range(NK):
                        mm(banks[g_b][:, 0:ncol], wg[:, kc, jj * 128:(jj + 1) * 128], hT[:, kc, 0:ncol],
                           kc == 0, kc == NK - 1, [bg, Bh[kc]], regs(g_b, 0, ncol), kc == NK - 1)
                    for kc in range(NK):
                        mm(banks[u_b][:, 0:ncol], wu[:, kc, jj * 128:(jj + 1) * 128], hT[:, kc, 0:ncol],
                           kc == 0, kc == NK - 1, [bu, Bh[kc]], regs(u_b, 0, ncol), kc == NK - 1)
                    si = nxt("sg", 2)
                    act(sgb[:, si, 0:ncol], banks[g_b][:, 0:ncol], AF.Silu, regs(g_b, 0, ncol), [Bsg[si]])
                    dve("tensor_tensor", dict(out=aT[:, j, 0:ncol], in0=banks[u_b][:, 0:ncol],
                                              in1=sgb[:, si, 0:ncol], op=ALU.mult),
                        regs(u_b, 0, ncol) + [Bsg[si]], [Ba[j]])
                releaseA()
                releaseA()
            for m in range(NK):
                wd, bd = acquireB()
                y_b = nextbank("c")
                for j in range(NF):
                    mm(banks[y_b][:, 0:ncol], wd[:, j, :], aT[:, j, 0:ncol], j == 0, j == NF - 1,
                       [bd, Ba[j]], regs(y_b, 0, ncol), j == NF - 1)
                dve("scalar_tensor_tensor",
                    dict(out=xres[:, m, 0:ncol], in0=banks[y_b][:, 0:ncol], scalar=0.5, op0=ALU.mult,
                         in1=xres[:, m, 0:ncol], op1=ALU.add),
                    regs(y_b, 0, ncol) + [Bx[m]], [Bx[m]])
                releaseB()

        def gelu_from_psum(zb, zc0, n, npart, out_ap, out_bufs):
            z = banks[zb][0:npart, zc0:zc0 + n]
            zr = regs(zb, zc0, n)
            i = nxt("st", NR)
            act(st512[0:npart, i, 0:n], z, AF.Square, zr, [Bst[i]])
            dve("tensor_scalar", dict(out=st512[0:npart, i, 0:n], in0=st512[0:npart, i, 0:n], scalar1=0.044715,
                                      scalar2=1.0, op0=ALU.mult, op1=ALU.add), [Bst[i]], [Bst[i]])
            dve("tensor_tensor", dict(out=st512[0:npart, i, 0:n], in0=z, in1=st512[0:npart, i, 0:n], op=ALU.mult),
                zr + [Bst[i]], [Bst[i]])
            act(st512[0:npart, i, 0:n], st512[0:npart, i, 0:n], AF.Sigmoid, [Bst[i]], [Bst[i]], scale=GELU_C)
            dve("tensor_tensor", dict(out=out_ap, in0=z, in1=st512[0:npart, i, 0:n], op=ALU.mult),
                zr + [Bst[i]], out_bufs)

        def stage(k):
            if dbg.get("mixstop") == k:
                raise _Stop()

        def _hoisted_diag(l):
            for w in range(31):
                if w % 2 == 0:
                    act(diag[:, w, :], identb[:], AF.Copy, CONST, [Bdiag[w]], scale=cpar(l, 0, w))
                else:
                    dve("tensor_scalar", dict(out=diag[:, w, :], in0=identb[:], scalar1=cpar(l, 0, w),
                                              scalar2=None, op0=ALU.mult), CONST, [Bdiag[w]])

        def mixing(l, ncol, kind, tile_idx):
            sample = kind == "s"
            groups = [(0, 64), (64, 64)] if sample else [(g * 128, 128) for g in range(4)]
            last_prompt = (not sample) and tile_idx == NTI - 1
            first_prompt = (not sample) and tile_idx == 0
            if not sample:
                _hoisted_diag(l)
            rmsnorm(l, 1, ncol)
            S.dma("pool", dict(out=biasb[:], in_=biasT_d[l]), dsem_for("biasb"), writes=[Bbias])
            act(biasb[:], biasb[:], AF.Copy, [Bbias], [Bbias], scale=8.0)
            if not sample:
                if not first_prompt:
                    for t in range(4):
                        S.op("act", "activation", dict(out=kT[l][:, :, t * 128:(t + 1) * 128],
                                                       in_=kT[l][:, :, 512 + t * 128:512 + (t + 1) * 128], func=AF.Copy),
                             reads=[BkT[l][4 + t]], writes=[BkT[l][t]])
                        S.op("act", "activation", dict(out=Vt[l][:, t, :], in_=Vt[l][:, 4 + t, :], func=AF.Copy),
                             reads=[BV[l][4 + t]], writes=[BV[l][t]])
                    S.op("act", "activation", dict(out=glu[l][:, :, 0:30], in_=glu[l][:, :, 512:542], func=AF.Copy),
                         reads=[Bglu[l]], writes=[Bglu[l]])
            else:
                for s in range(2):
                    for c in range(2):
                        S.dma("sp", dict(out=gluS[:, s, c, 0:30],
                                         in_=cconv[l, s, :, c * 128:(c + 1) * 128].rearrange("t p -> p t"),
                                         allow_slow_non_contiguous=True), dsem_for(f"gluS{s}"), writes=[BgluS[s]])

            stage(1)
            w0, b0 = acquireA(22 + 0)
            for c in range(2):
                zb = nextbank("a")
                for kc in range(NK):
                    mm(banks[zb][:, 0:ncol], w0[:, kc, c * 128:(c + 1) * 128], hT[:, kc, 0:ncol],
                       kc == 0, kc == NK - 1, [b0, Bh[kc]], regs(zb, 0, ncol), kc == NK - 1)
                act(zaT[:, c, 0:ncol], banks[zb][:, 0:ncol], AF.Copy, regs(zb, 0, ncol), [Bza])
            releaseA()
            w1, b1 = acquireA(22 + 1)
            for c in range(2):
                zb = nextbank("b")
                for kc in range(NK):
                    mm(banks[zb][:, 0:ncol], w1[:, kc, c * 128:(c + 1) * 128], hT[:, kc, 0:ncol],
                       kc == 0, kc == NK - 1, [b1, Bh[kc]], regs(zb, 0, ncol), kc == NK - 1)
                si = nxt("sg", 2)
                act(sgb[:, si, 0:ncol], banks[zb][:, 0:ncol], AF.Sigmoid, regs(zb, 0, ncol), [Bsg[si]])
                if not sample:
                    dve("tensor_tensor", dict(out=glu[l][:, c, 30:542], in0=zaT[:, c, 0:512], in1=sgb[:, si, 0:512],
                                              op=ALU.mult), [Bza, Bsg[si]], [Bglu[l]])
                else:
                    for s in range(2):
                        dve("tensor_tensor", dict(out=gluS[:, s, c, 30:94], in0=zaT[:, c, s * 64:(s + 1) * 64],
                                                  in1=sgb[:, si, s * 64:(s + 1) * 64], op=ALU.mult),
                            [Bza, Bsg[si]], [BgluS[s]])
            releaseA()
            if last_prompt:
                for c in range(2):
                    out_tickets.append(S.dma("sp", dict(out=pconv[l, :, c * 128:(c + 1) * 128].rearrange("t p -> p t"),
                                                        in_=glu[l][:, c, 512:542], allow_slow_non_contiguous=True),
                                             dsem_for(f"st_glu{l}"), reads=[Bglu[l]]))
            if sample:
                for s in range(2):
                    for c in range(2):
                        out_tickets.append(S.dma("sp", dict(out=sconv[l, s, :, c * 128:(c + 1) * 128].rearrange("t p -> p t"),
                                                            in_=gluS[:, s, c, 64:94], allow_slow_non_contiguous=True),
                                                 dsem_for(f"st_gluS{s}"), reads=[BgluS[s]]))

            stage(2)
            deferred = []
            segs = [(0, 512, None)] if not sample else [(0, 64, 0), (64, 64, 1)]

            def conv_prep(seg, c):
                (c0, n, s) = seg
                src_all = glu[l][:, c, 0:30 + n] if s is None else gluS[:, s, c, 0:30 + n]
                sb = Bglu[l] if s is None else BgluS[s]
                act(gluB[:, c, 0:30 + n], src_all, AF.Copy, [sb], [BgluB[c]])
                if c == 0 and s is None:
                    return
                conv_diag(c)

            def conv_diag(c):
                for w in range(31):
                    if w % 2 == 0:
                        act(diag[:, w, :], identb[:], AF.Copy, CONST, [Bdiag[w]], scale=cpar(l, c, w))
                    else:
                        dve("tensor_scalar", dict(out=diag[:, w, :], in0=identb[:], scalar1=cpar(l, c, w),
                                                  scalar2=None, op0=ALU.mult), CONST, [Bdiag[w]])

            def conv_mm(seg, c, pool="c"):
                (c0, n, s) = seg
                yb = nextbank(pool)
                for w in range(31):
                    mm(banks[yb][:, 0:n], diag[:, w, :], gluB[:, c, w:w + n], w == 0, w == 30,
                       [Bdiag[w], BgluB[c]], [Bbank[yb]], w == 30)
                act(cacc[:, c, c0:c0 + n], banks[yb][:, 0:n], AF.Identity, [Bbank[yb]] + CONST, [Bcacc[c]],
                    bias=cpar(l, c, 31))

            def conv_ln(seg, p1="c", p2="d"):
                (c0, n, s) = seg
                act(csq[:, :, c0:c0 + n], cacc[:, :, c0:c0 + n], AF.Square, Bcacc, [Bcsq])
                b1_ = nextbank(p1)
                b2_ = nextbank(p2)
                for c in range(2):
                    mm(banks[b1_][:, 0:n], onesf[:], cacc[:, c, c0:c0 + n], c == 0, c == 1, [Bcacc[c]] + CONST,
                       regs(b1_, 0, n), c == 1)
                for c in range(2):
                    mm(banks[b2_][:, 0:n], onesf[:], csq[:, c, c0:c0 + n], c == 0, c == 1, [Bcsq] + CONST,
                       regs(b2_, 0, n), c == 1)
                im = nxt("st", NR)
                dve("tensor_scalar", dict(out=st512[:, im, 0:n], in0=banks[b1_][:, 0:n], scalar1=1.0 / 256.0,
                                          scalar2=None, op0=ALU.mult), regs(b1_, 0, n), [Bst[im]])
                iq = nxt("st", NR)
                dve("tensor_tensor", dict(out=st512[:, iq, 0:n], in0=st512[:, im, 0:n], in1=st512[:, im, 0:n],
                                          op=ALU.mult), [Bst[im]], [Bst[iq]])
                dve("scalar_tensor_tensor", dict(out=st512[:, iq, 0:n], in0=banks[b2_][:, 0:n], scalar=1.0 / 256.0,
                                                 op0=ALU.mult, in1=st512[:, iq, 0:n], op1=ALU.subtract),
                    regs(b2_, 0, n) + [Bst[iq]], [Bst[iq]])
                act(st512[:, iq, 0:n], st512[:, iq, 0:n], AF.Sqrt, [Bst[iq]] + CONST, [Bst[iq]], bias=epsT[:, 0:1])
                dve("reciprocal", dict(out=st512[:, iq, 0:n], in_=st512[:, iq, 0:n]), [Bst[iq]], [Bst[iq]])
                for c in range(2):
                    dve("tensor_tensor", dict(out=cacc[:, c, c0:c0 + n], in0=cacc[:, c, c0:c0 + n], in1=st512[:, im, 0:n],
                                              op=ALU.subtract), [Bcacc[c], Bst[im]], [Bcacc[c]])
                    dve("tensor_tensor", dict(out=cacc[:, c, c0:c0 + n], in0=cacc[:, c, c0:c0 + n], in1=st512[:, iq, 0:n],
                                              op=ALU.mult), [Bcacc[c], Bst[iq]], [Bcacc[c]])
                    deferred.append((c, c0, n))

            def slab2_mm():
                w2, b2 = acquireA(22 + 2)
                zbs = []
                for c in range(2):
                    zb = nextbank("a")
                    for kc in range(NK):
                        mm(banks[zb][:, 0:ncol], w2[:, kc, c * 128:(c + 1) * 128], hT[:, kc, 0:ncol],
                           kc == 0, kc == NK - 1, [b2, Bh[kc]], regs(zb, 0, ncol), kc == NK - 1)
                    zbs.append(zb)
                releaseA()
                return zbs

            def slab2_gelu(zbs):
                for c in range(2):
                    gelu_from_psum(zbs[c], 0, ncol, 128, uT[:, c, 0:ncol], [BuT])

            if sample:
                for seg in segs:
                    conv_prep(seg, 0)
                    conv_mm(seg, 0)
                    conv_prep(seg, 1)
                    conv_mm(seg, 1)
                    conv_ln(seg)
                stage(3)
                slab2_gelu(slab2_mm())
            else:
                seg = segs[0]
                conv_prep(seg, 0)
                stage(3)
                zbs2 = slab2_mm()
                conv_mm(seg, 0)
                conv_prep(seg, 1)
                slab2_gelu(zbs2)

            stage(4)
            tp = l * 640
            ng = len(groups)

            def tok_matmuls(ws_, bs_):
                zs = []
                for gi, (g0, n) in enumerate(groups):
                    zb = nextbank("w")
                    for kc in range(NK):
                        mm(banks[zb][0:n, 0:256], hT[:, kc, g0:g0 + n], ws_[:, kc, :], kc == 0, kc == NK - 1,
                           [bs_, Bh[kc]], [Bbank[zb]], kc == NK - 1)
                    zs.append(zb)
                return zs

            w3, b3 = acquireA(22 + 3)
            zs = tok_matmuls(w3, b3)
            releaseA()
            if not sample:
                conv_mm(segs[0], 1, "d")
                conv_ln(segs[0], "d", "d")
            G = list(enumerate(groups))

            def zz(gi):
                return banks[zs[gi]][0:groups[gi][1], 0:256]
            for gi, (g0, n) in G:
                act(tk256[0:n, gi, :], zz(gi), AF.Square, [Bbank[zs[gi]]], [Btk[gi]])
            for gi, (g0, n) in G:
                dve("tensor_scalar", dict(out=tk256[0:n, gi, :], in0=tk256[0:n, gi, :], scalar1=0.044715, scalar2=1.0,
                                          op0=ALU.mult, op1=ALU.add), [Btk[gi]], [Btk[gi]])
            for gi, (g0, n) in G:
                dve("tensor_tensor", dict(out=tk256[0:n, gi, :], in0=zz(gi), in1=tk256[0:n, gi, :], op=ALU.mult),
                    [Bbank[zs[gi]], Btk[gi]], [Btk[gi]])
            for gi, (g0, n) in G:
                act(tk256[0:n, gi, :], tk256[0:n, gi, :], AF.Sigmoid, [Btk[gi]], [Btk[gi]], scale=GELU_C)
            for gi, (g0, n) in G:
                dve("tensor_tensor", dict(out=tk256[0:n, gi, :], in0=zz(gi), in1=tk256[0:n, gi, :], op=ALU.mult),
                    [Bbank[zs[gi]], Btk[gi]], [Btk[gi]])
            for gi, (g0, n) in G:
                dve("bn_stats", dict(out=bnst[0:n, gi, :], in_=tk256[0:n, gi, :]), [Btk[gi]], [Bbn[gi]])
            for gi, (g0, n) in G:
                dve("bn_aggr", dict(out=small[0:n, gi, 0:2], in_=bnst[0:n, gi, :]), [Bbn[gi]], [Bsmall[gi]])
            for gi, (g0, n) in G:
                act(small[0:n, gi, 2:3], small[0:n, gi, 1:2], AF.Sqrt, [Bsmall[gi]] + CONST, [Bsmall[gi]],
                    bias=epsT[0:n, 0:1])
            for gi, (g0, n) in G:
                dve("reciprocal", dict(out=small[0:n, gi, 3:4], in_=small[0:n, gi, 2:3]), [Bsmall[gi]], [Bsmall[gi]])
            for gi, (g0, n) in G:
                dve("tensor_scalar", dict(out=tk256[0:n, gi, :], in0=tk256[0:n, gi, :], scalar1=small[0:n, gi, 0:1],
                                          scalar2=small[0:n, gi, 3:4], op0=ALU.subtract, op1=ALU.mult),
                    [Btk[gi], Bsmall[gi]], [Btk[gi]])
            for gi, (g0, n) in G:
                dve("tensor_tensor", dict(out=tk256[0:n, gi, :], in0=tk256[0:n, gi, :], in1=tokpT[0:n, tp:tp + 256],
                                          op=ALU.mult), [Btk[gi]] + CONST, [Btk[gi]])
            for gi, (g0, n) in G:
                dve("tensor_tensor", dict(out=tk256[0:n, gi, :], in0=tk256[0:n, gi, :],
                                          in1=tokpT[0:n, tp + 256:tp + 512], op=ALU.add), [Btk[gi]] + CONST, [Btk[gi]])
            for gi, (g0, n) in G:
                act(vgb[0:n, gi, :], tk256[0:n, gi, :], AF.Copy, [Btk[gi]], [Bvgb[gi]])
                if sample:
                    out_tickets.append(S.dma("sp", dict(out=sgv[l, gi], in_=tk256[0:n, gi, :]), dsem_for(f"st_tk{gi}"),
                                             reads=[Btk[gi]]))

            stage(5)
            pending = []

            def flush_pending():
                for f in pending:
                    f()
                del pending[:]
            for si_ in range(6):
                ws_, bs_ = acquireA(22 + 4 + si_)
                which = si_ // 2
                half_s = si_ % 2
                zs = tok_matmuls(ws_, bs_)
                releaseA()
                flush_pending()

                def zz(gi, zs=zs):
                    return banks[zs[gi]][0:groups[gi][1], 0:256]
                if which == 2:
                    for gi, (g0, n) in G:
                        zr = [Bbank[zs[gi]]]
                        if sample:
                            act(Vs_new[0:n, gi, half_s * 256:(half_s + 1) * 256], zz(gi), AF.Copy, zr, [BVn[gi]])
                        else:
                            act(Vt[l][:, 4 + gi, half_s * 256:(half_s + 1) * 256], zz(gi), AF.Copy, zr, [BV[l][4 + gi]])
                        if sample or last_prompt:
                            act(tk256[0:n, gi, :], zz(gi), AF.Copy, zr, [Btk[gi]])
                            dst = sv[l, gi, :, half_s * 256:(half_s + 1) * 256] if sample else \
                                pv[l, gi * 128:(gi + 1) * 128, half_s * 256:(half_s + 1) * 256]
                            out_tickets.append(S.dma("sp", dict(out=dst, in_=tk256[0:n, gi, :]), dsem_for(f"st_tk{gi}"),
                                                     reads=[Btk[gi]]))
                    continue
                for gi, (g0, n) in G:
                    act(tk256[0:n, gi, :], zz(gi), AF.Square, [Bbank[zs[gi]]], [Btk[gi]])
                for gi, (g0, n) in G:
                    dve("tensor_reduce", dict(out=small[0:n, gi, 0:4],
                                              in_=tk256[0:n, gi, :].rearrange("p (h d) -> p h d", d=64),
                                              axis=AX.X, op=ALU.add), [Btk[gi]], [Bsmall[gi]])
                for gi, (g0, n) in G:
                    act(small[0:n, gi, 0:4], small[0:n, gi, 0:4], AF.Sqrt, [Bsmall[gi]] + CONST, [Bsmall[gi]],
                        scale=1.0 / 64.0, bias=epsT[0:n, 0:1])
                for gi, (g0, n) in G:
                    dve("reciprocal", dict(out=small[0:n, gi, 4:8], in_=small[0:n, gi, 0:4]), [Bsmall[gi]], [Bsmall[gi]])
                for gi, (g0, n) in G:
                    dve("tensor_tensor", dict(out=tk256[0:n, gi, :].rearrange("p (h d) -> p h d", d=64),
                                              in0=zz(gi).rearrange("p (h d) -> p h d", d=64),
                                              in1=small[0:n, gi, 4:8].unsqueeze(2).to_broadcast([n, 4, 64]),
                                              op=ALU.mult), [Bbank[zs[gi]], Bsmall[gi]], [Btk[gi]])
                go = tp + 512 + which * 64
                for gi, (g0, n) in G:
                    dve("tensor_tensor", dict(out=tk256[0:n, gi, :].rearrange("p (h d) -> p h d", d=64),
                                              in0=tk256[0:n, gi, :].rearrange("p (h d) -> p h d", d=64),
                                              in1=tokpT[0:n, go:go + 64].unsqueeze(1).to_broadcast([n, 4, 64]),
                                              op=ALU.mult), [Btk[gi]] + CONST, [Btk[gi]])
                for gi, (g0, n) in G:
                    act(tkb[0:n, gi, :], tk256[0:n, gi, :], AF.Copy, [Btk[gi]], [Btkb[gi]])
                    if which == 1 and (sample or last_prompt):
                        dst = sk[l, gi, :, half_s * 256:(half_s + 1) * 256] if sample else \
                            pk[l, gi * 128:(gi + 1) * 128, half_s * 256:(half_s + 1) * 256]
                        out_tickets.append(S.dma("sp", dict(out=dst, in_=tk256[0:n, gi, :]), dsem_for(f"st_tk{gi}"),
                                                 reads=[Btk[gi]]))

                def transposes(which=which, half_s=half_s):
                    for gi, (g0, n) in G:
                        pb = nextbank("d")
                        pview = banks[pb][:, 0:128].bitcast(BF16)
                        for cc in range(2):
                            tr(pview[:, cc * 128:cc * 128 + n], tkb[0:n, gi, cc * 128:(cc + 1) * 128], identb[0:n, 0:n],
                               [Btkb[gi]] + CONST, [Bbank[pb]], cc == 1)
                        pv3 = pview.rearrange("p (c t) -> p c t", t=128)[:, :, 0:n]
                        ch0 = half_s * 2
                        if which == 0:
                            qb_ = [BqT[gi if not sample else 0]]
                            act(qTz[0:64, 0, ch0:ch0 + 2, g0:g0 + n], pv3[0:64], AF.Copy, [Bbank[pb]], qb_)
                            act(qTz[64:128, 1, ch0:ch0 + 2, g0:g0 + n], pv3[64:128], AF.Copy, [Bbank[pb]], qb_)
                        elif not sample:
                            act(kT[l][:, ch0:ch0 + 2, 512 + g0:512 + g0 + n], pv3, AF.Copy, [Bbank[pb]],
                                [BkT[l][4 + gi]])
                        else:
                            act(kTs_new[:, gi, ch0:ch0 + 2, 0:64], pv3, AF.Copy, [Bbank[pb]], [BkTn[gi]])
                if not dbg.get("no_tr"):
                    pending.append(transposes)
            flush_pending()

            stage(6)
            for (c, c0, n) in deferred:
                act(mixT[:, c, c0:c0 + n], cacc[:, c, c0:c0 + n], AF.Silu, [Bcacc[c]] + CONST, [Bm[c]],
                    scale=cpar(l, c, 32), bias=cpar(l, c, 33))

            for gi, (g0, n) in enumerate(groups):
                for c in range(2):
                    zb = nextbank("c")
                    for half in range(2):
                        h = 2 * c + half
                        mm(banks[zb][half * 64:(half + 1) * 64, 0:n], vgb[0:n, gi, h * 64:(h + 1) * 64],
                           WsT[0:n, l * 4 + h, 0:n], True, True, [Bvgb[gi], BWs], regs(zb, 0, n), half == 1)
                    i = nxt("st", NR)
                    go = (l * 2 + c) * 128
                    dve("tensor_tensor", dict(out=st512[:, i, 0:n], in0=banks[zb][:, 0:n], in1=gbiasT[:, go:go + n],
                                              op=ALU.add), regs(zb, 0, n) + CONST, [Bst[i]])
                    dve("tensor_tensor", dict(out=mixT[:, 2 + c, g0:g0 + n], in0=st512[:, i, 0:n],
                                              in1=uT[:, c, g0:g0 + n], op=ALU.mult), [Bst[i], BuT], [Bm[2 + c]])

            stage(7)
            MSLOT = {0: 0, 3: 1, 4: 2}
            CSLOT = {1: 0, 2: 1}

            def attn_unit_scores(u):
                par = u["seq"] % 3
                mb, cb = 2 * par, 2 * par + 1
                nq = u["nq"]
                keys = u["keys"]
                h = u["h"]
                for idx, (k_ap, v_ap, kb, vb, j) in enumerate(keys):
                    last = idx == len(keys) - 1
                    if j in MSLOT:
                        b, col = mb, MSLOT[j] * 128
                        bo = (h * 3 + MSLOT[j]) * 128
                        mm(banks[b][:, col:col + nq], k_ap, u["q"], True, False, [kb, u["qb"]], [Bbank[b]], False)
                        mm(banks[b][:, col:col + nq], identb[:], biasb[:, bo:bo + nq], False, True,
                           [Bbias] + CONST, [Bbank[b]], last)
                    else:
                        b, col = cb, CSLOT[j] * 128
                        mm(banks[b][:, col:col + nq], k_ap, u["q"], True, True, [kb, u["qb"]], [Bbank[b]], last)
                js = [k[4] for k in keys]
                m0 = min([MSLOT[j] for j in js if j in MSLOT])
                cs = [CSLOT[j] for j in js if j in CSLOT]
                act(PT[:, par, m0:3, 0:nq], banks[mb][:, 0:384].rearrange("p (t q) -> p t q", q=128)[:, m0:3, 0:nq],
                    AF.Exp, [Bbank[mb]], [BPT[par]], scale=0.125)
                if cs:
                    c0_ = min(cs)
                    act(PT[:, par, 3 + c0_:5, 0:nq],
                        banks[cb][:, 0:256].rearrange("p (t q) -> p t q", q=128)[:, c0_:2, 0:nq], AF.Exp,
                        [Bbank[cb]] + CONST, [BPT[par]], scale=0.125, bias=chbT[:, l * 8 + h:l * 8 + h + 1])

            def attn_unit_pv(u):
                par = u["seq"] % 3
                hb = (u["h"] % 2) * 64
                nq = u["nq"]
                keys = u["keys"]
                nkeys = len(keys)

                def slot(j):
                    return MSLOT[j] if j in MSLOT else 3 + CSLOT[j]
                for idx, (k_ap, v_ap, kb, vb, j) in enumerate(keys):
                    mm(banks[u["ob"]][hb:hb + 64, 0:nq], v_ap, PT[:, par, slot(j), 0:nq], idx == 0,
                       idx == nkeys - 1, [vb, BPT[par]], [Bbank[u["ob"]]], False)
                for idx, (k_ap, v_ap, kb, vb, j) in enumerate(keys):
                    mm(banks[u["db"]][hb:hb + 64, 128:128 + nq], onesb[:, 0:64], PT[:, par, slot(j), 0:nq], idx == 0,
                       idx == nkeys - 1, [BPT[par]] + CONST, [Bbank[u["db"]]], idx == nkeys - 1)
                if u["h"] % 2 == 1:
                    i = nxt("st", NR)
                    dve("reciprocal", dict(out=st512[:, i, 0:nq], in_=banks[u["db"]][:, 128:128 + nq]),
                        [Bbank[u["db"]]], [Bst[i]])
                    dve("tensor_tensor", dict(out=mixT[:, 4 + u["h"] // 2, u["c0"]:u["c0"] + nq],
                                              in0=banks[u["ob"]][:, 0:nq], in1=st512[:, i, 0:nq], op=ALU.mult),
                        [Bbank[u["ob"]], Bst[i]], [Bm[4 + u["h"] // 2]])

            def run_units(units):
                for i, u in enumerate(units):
                    u["seq"] = useq[0]
                    useq[0] += 1
                    attn_unit_scores(u)
                    if i >= 2:
                        attn_unit_pv(units[i - 2])
                for u in units[-2:]:
                    attn_unit_pv(u)

            if not sample and dbg.get("no_attn_p"):
                pass
            elif sample and dbg.get("no_attn_s"):
                pass
            elif not sample:
                units = []
                for p in range(4):
                    jmin = max(0, 4 - p) if first_prompt else dbg.get("jmin", 0)
                    for c in range(4):
                        ob = db = nextbank("d")
                        for half in range(2):
                            h = 2 * c + half
                            hb = half * 64
                            keys = []
                            for j in range(jmin, 5):
                                kt = p + j
                                keys.append((kT[l][:, c, kt * 128:(kt + 1) * 128],
                                             Vt[l][:, kt, h * 64:(h + 1) * 64], BkT[l][kt], BV[l][kt], j))
                            units.append(dict(h=h, nq=128, c0=p * 128, q=qTz[:, half, c, p * 128:(p + 1) * 128],
                                              qb=BqT[p], keys=keys, ob=ob, db=db))
                run_units(units)
            else:
                allprompt = [b_ for l_ in range(L) for b_ in BkT[l_]] + [b_ for l_ in range(L) for b_ in BV[l_]]
                for s in range(2):
                    S.dma("pool", dict(out=cks, in_=ck[l, s].rearrange("(t p) f -> p t f", p=128)),
                          dsem_for("cks"), writes=[Bcks])
                    S.dma("pool", dict(out=Vs[:, 0:4, :], in_=cv[l, s].rearrange("(t p) f -> p t f", p=128)),
                          dsem_for("Vs"), writes=BVs[0:4] + allprompt)
                    for t in range(4):
                        for c2 in range(2):
                            pb = nextbank("d")
                            pview = banks[pb][:, 0:128].bitcast(BF16)
                            for cc in range(2):
                                tr(pview[:, cc * 128:(cc + 1) * 128], cks[:, t, (c2 * 2 + cc) * 128:(c2 * 2 + cc + 1) * 128],
                                   identb[:], [Bcks] + CONST, regs(pb, 0, 128), cc == 1)
                            act(kTs[:, c2 * 2:c2 * 2 + 2, t * 128:(t + 1) * 128],
                                pview.rearrange("p (c t) -> p c t", t=128), AF.Copy, regs(pb, 0, 128),
                                [BkTs[t]] + allprompt)
                    units = []
                    for c in range(4):
                        ob = db = nextbank("d")
                        for half in range(2):
                            h = 2 * c + half
                            hb = half * 64
                            keys = []
                            for j in range(4):
                                keys.append((kTs[:, c, j * 128:(j + 1) * 128], Vs[:, j, h * 64:(h + 1) * 64],
                                             BkTs[j], BVs[j], j))
                            keys.append((kTs_new[:, s, c, :], Vs_new[:, s, h * 64:(h + 1) * 64],
                                         BkTn[s], BVn[s], 4))
                            units.append(dict(h=h, nq=64, c0=s * 64, q=qTz[:, half, c, s * 64:(s + 1) * 64],
                                              qb=BqT[0], keys=keys, ob=ob, db=db))
                    run_units(units)

            stage(8)
            for s4 in range(4):
                wo, bo_ = acquireA(22 + 10 + s4)
                for mm_ in range(2):
                    m = 2 * s4 + mm_
                    yb = nextbank("a" if mm_ == 0 else "b")
                    for kc in range(NK):
                        mm(banks[yb][:, 0:ncol], wo[:, kc, mm_ * 128:(mm_ + 1) * 128], mixT[:, kc, 0:ncol],
                           kc == 0, kc == NK - 1, [bo_, Bm[kc]], regs(yb, 0, ncol), kc == NK - 1)
                    dve("tensor_tensor", dict(out=xres[:, m, 0:ncol], in0=banks[yb][:, 0:ncol], in1=xres[:, m, 0:ncol],
                                              op=ALU.add), regs(yb, 0, ncol) + [Bx[m]], [Bx[m]])
                releaseA()

        kTs_new = T("kTs_new", [128, 2, 4, 128], BF16)
        BkTn = [Buf("kTn0"), Buf("kTn1")]
        dve("memset", dict(ap=kTs_new[:], constant=0.0), [], BkTn)
        dve("memset", dict(ap=qTz[:], constant=0.0), [], BqT)
        dve("memset", dict(ap=Vs_new[:], constant=0.0), [], BVn)

        def load_tile(kind, tile_idx):
            groups = [(0, 64), (64, 64)] if kind == "s" else [(g * 128, 128) for g in range(4)]
            for gi, (g0, n) in enumerate(groups):
                xi = nxt("xin", 1)
                src = xs[g0:g0 + n, :] if kind == "s" else xp[tile_idx * 512 + g0:tile_idx * 512 + g0 + n, :]
                S.dma("sp", dict(out=xin[0:n, xi, :], in_=src), sem_xin[xi], writes=[Bxin[xi]])
                for hh in range(2):
                    b = nextbank("c" if hh == 0 else "d")
                    for q4 in range(4):
                        kc = hh * 4 + q4
                        tr(banks[b][:, q4 * 128:q4 * 128 + n], xin[0:n, xi, kc * 128:(kc + 1) * 128], identf[0:n, 0:n],
                           [Bxin[xi]] + CONST, [Bbank[b]], q4 == 3)
                    src3 = banks[b][:, :].rearrange("p (c t) -> p c t", t=128)[:, :, 0:n]
                    if hh == 0:
                        act(xres[:, 0:4, g0:g0 + n], src3, AF.Copy, bregs[b], Bx[0:4])
                    else:
                        dve("tensor_copy", dict(out=xres[:, 4:8, g0:g0 + n], in_=src3), bregs[b], Bx[4:8])

        def store_tile(kind, tile_idx):
            groups = [(0, 64), (64, 64)] if kind == "s" else [(g * 128, 128) for g in range(4)]
            for gi, (g0, n) in enumerate(groups):
                xo = nxt("xout", 1)
                for hh in range(2):
                    b = nextbank("c" if hh == 0 else "d")
                    for q4 in range(4):
                        kc = hh * 4 + q4
                        tr(banks[b][0:n, q4 * 128:(q4 + 1) * 128], xres[:, kc, g0:g0 + n], identf[:, :],
                           [Bx[kc]] + CONST, [Bbank[b]], q4 == 3)
                    if hh == 0:
                        act(xout[0:n, xo, 0:512], banks[b][0:n, :], AF.Copy, bregs[b], [Bxout[xo]])
                    else:
                        dve("tensor_copy", dict(out=xout[0:n, xo, 512:1024], in_=banks[b][0:n, :]), bregs[b], [Bxout[xo]])
                dst = ys[g0:g0 + n, :] if kind == "s" else yp[tile_idx * 512 + g0:tile_idx * 512 + g0 + n, :]
                out_tickets.append(S.dma("sp", dict(out=dst, in_=xout[0:n, xo, :]), sem_out[xo], reads=[Bxout[xo]]))

        tiles = [("p", t) for t in range(NTI)] + [("s", 0)]
        for (kind, ti) in tiles:
            ncol = 128 if kind == "s" else 512
            load_tile(kind, ti)
            for l in range(dbg.get("layers", L)):
                if "ffn1" in dbg.get("phases", ("ffn1", "mix", "ffn2")):
                    ffn(l, 0, ncol)
                else:
                    stA["con"] += 22; stB["con"] += 8; prefetchA(); prefetchB()
                if "mix" in dbg.get("phases", ("ffn1", "mix", "ffn2")):
                    a0 = stA["con"]
                    try:
                        mixing(l, ncol, kind, ti)
                    except _Stop:
                        stA["con"] = a0 + 14
                        prefetchA()
                else:
                    stA["con"] += 14; prefetchA()
                if "ffn2" in dbg.get("phases", ("ffn1", "mix", "ffn2")):
                    ffn(l, 1, ncol)
                else:
                    stA["con"] += 22; stB["con"] += 8; prefetchA(); prefetchB()
            for l in range(dbg.get("layers", L), L):
                stA["con"] += NSLAB_A; stB["con"] += NSLAB_B; prefetchA(); prefetchB()
            store_tile(kind, ti)

        S.wait_all("sp", out_tickets)
        with nc.Block() as block:
            S.emit(block)
    return nc


def _slabA(w):
    C = w.shape[1]
    return np.ascontiguousarray(w.reshape(8, 128, C // 256, 256).transpose(2, 1, 0, 3)).reshape(C // 256, 128, 2048)


def _slabB(w):
    return np.ascontiguousarray(w.reshape(22, 128, 8, 128).transpose(2, 1, 0, 3)).reshape(8, 128, 2816)


def prepare_shared(inp):
    f = np.float32
    wA = np.empty((L * NSLAB_A, 128, 2048), f)
    wB = np.empty((L * NSLAB_B, 128, 2816), f)
    for l in range(L):
        o = l * NSLAB_A
        for w, base in ((0, 0), (1, 36)):
            g = _slabA(np.asarray(inp[f"ffn{w + 1}_w_gate"][l]))
            u = _slabA(np.asarray(inp[f"ffn{w + 1}_w_up"][l]))
            for s in range(11):
                wA[o + base + 2 * s] = g[s]
                wA[o + base + 2 * s + 1] = u[s]
            wB[l * NSLAB_B + w * 8:l * NSLAB_B + w * 8 + 8] = _slabB(np.asarray(inp[f"ffn{w + 1}_w_down"][l]))
        wA[o + 22:o + 32] = _slabA(np.asarray(inp["w_in"][l]))
        wA[o + 32:o + 36] = _slabA(np.asarray(inp["w_out"][l]))
    gains = np.empty((128, L, 3, 8), f)
    for l in range(L):
        for i, nm in enumerate(("ffn1_norm", "mix_norm", "ffn2_norm")):
            gains[:, l, i, :] = np.asarray(inp[nm][l]).reshape(8, 128).T
    convp = np.empty((128, L, 2, 34), f)
    for l in range(L):
        cw = np.asarray(inp["conv_w"][l])
        for c in range(2):
            convp[:, l, c, 0:31] = cw[:, c * 128:(c + 1) * 128].T
            convp[:, l, c, 31] = np.asarray(inp["conv_b"][l])[c * 128:(c + 1) * 128]
            convp[:, l, c, 32] = np.asarray(inp["conv_ln_g"][l])[c * 128:(c + 1) * 128]
            convp[:, l, c, 33] = np.asarray(inp["conv_ln_b"][l])[c * 128:(c + 1) * 128]
    tokp = np.empty((128, L, 640), f)
    for l in range(L):
        tokp[:, l, 0:256] = np.asarray(inp["gmlp_ln_g"][l])[None, :]
        tokp[:, l, 256:512] = np.asarray(inp["gmlp_ln_b"][l])[None, :]
        tokp[:, l, 512:576] = np.asarray(inp["q_norm"][l])[None, :]
        tokp[:, l, 576:640] = np.asarray(inp["k_norm"][l])[None, :]
    gb = np.asarray(inp["gmlp_b"])
    gbias = np.empty((128, L, 2, 128), f)
    for l in range(L):
        for c in range(2):
            gbias[0:64, l, c, :] = gb[l, 2 * c][None, :]
            gbias[64:128, l, c, :] = gb[l, 2 * c + 1][None, :]
    rb = np.asarray(inp["rel_bias"])
    chb = np.empty((128, L, 8), f)
    chb[:] = rb[None, :, :, 256]
    kl = np.arange(128)[:, None]
    ql = np.arange(128)[None, :]
    biasT = np.empty((L, 128, 8, 3, 128), f)
    for ji, j in enumerate((0, 3, 4)):
        rel = ql - kl + (4 - j) * 128
        idx = np.clip(rel, -128, 128) + 128
        tab = rb[:, :, idx]
        if j == 4:
            msk = (kl >= 64) & (ql < 64)
        elif j == 0:
            msk = (kl < 64) & (ql >= 64)
        else:
            msk = np.zeros((128, 128), bool)
        tab = np.where(msk[None, None], f(-30000.0), tab)
        biasT[:, :, :, ji, :] = tab.transpose(0, 2, 1, 3)
    return dict(
        wA=wA, wB=wB, gains=gains.reshape(128, -1), convp=convp.reshape(128, -1), tokp=tokp.reshape(128, -1),
        gbias=gbias.reshape(128, -1), gws=np.ascontiguousarray(np.asarray(inp["gmlp_ws"], f).reshape(L * 4, 128, 128)),
        chb=chb.reshape(128, -1), biasT=biasT.reshape(L, 128, -1), ident=np.eye(128, dtype=f),
        triu=np.triu(np.ones((128, 128), f)),
    )


def run(inp, NTI):
    SEQ = 512 * (2 * NTI - 2)
    x_prompt = np.asarray(inp["x_prompt"], np.float32)
    x_sample = np.asarray(inp["x_sample"], np.float32)
    B = x_prompt.shape[0]
    assert x_prompt.shape[1] == SEQ and B * 2 == 8 and x_sample.shape[0] == 16
    shared = prepare_shared(inp)
    cconv = np.asarray(inp["cache_conv"], np.float32)
    ck = np.asarray(inp["cache_k"], np.float32).reshape(L, 16, 512, 512)
    cv = np.asarray(inp["cache_v"], np.float32).reshape(L, 16, 512, 512)
    in_maps = []
    for c in range(8):
        b, half = c // 2, c % 2
        t0 = 0 if half == 0 else SEQ - NTI * 512
        m = dict(shared)
        m["xp"] = np.ascontiguousarray(x_prompt[b, t0:t0 + NTI * 512])
        m["xs"] = np.ascontiguousarray(x_sample[2 * c:2 * c + 2].reshape(128, 1024))
        m["cconv"] = np.ascontiguousarray(cconv[:, 2 * c:2 * c + 2])
        m["ck"] = np.ascontiguousarray(ck[:, 2 * c:2 * c + 2])
        m["cv"] = np.ascontiguousarray(cv[:, 2 * c:2 * c + 2])
        in_maps.append(m)
    nc = build_program(NTI)
    res = run_bass_kernel_spmd(nc, in_maps, core_ids=list(range(8)))
    R = res.results
    f = np.float32
    y_prompt = np.empty((B, SEQ, 1024), f)
    y_sample = np.empty((16, 64, 1024), f)
    p_conv = np.empty((L, B, 30, 256), f)
    p_k = np.empty((L, B, 512, 8, 64), f)
    p_v = np.empty((L, B, 512, 8, 64), f)
    s_conv = np.empty((L, 16, 30, 256), f)
    s_k = np.empty((L, 16, 64, 8, 64), f)
    s_v = np.empty((L, 16, 64, 8, 64), f)
    s_gv = np.empty((L, 16, 64, 4, 64), f)
    for c in range(8):
        b, half = c // 2, c % 2
        r = R[c]
        if half == 0:
            y_prompt[b, 0:NTI * 512] = r["yp"]
        else:
            y_prompt[b, NTI * 512:SEQ] = r["yp"][1024:]
            p_conv[:, b] = r["pconv"]
            p_k[:, b] = r["pk"].reshape(L, 512, 8, 64)
            p_v[:, b] = r["pv"].reshape(L, 512, 8, 64)
        y_sample[2 * c:2 * c + 2] = r["ys"].reshape(2, 64, 1024)
        s_conv[:, 2 * c:2 * c + 2] = r["sconv"]
        s_k[:, 2 * c:2 * c + 2] = r["sk"].reshape(L, 2, 64, 8, 64)
        s_v[:, 2 * c:2 * c + 2] = r["sv"].reshape(L, 2, 64, 8, 64)
        s_gv[:, 2 * c:2 * c + 2] = r["sgv"].reshape(L, 2, 64, 4, 64)
    return (y_prompt, y_sample, p_conv, p_k, p_v, s_conv, s_k, s_v, s_gv)


def kernel(**inputs):
    return run(inputs, NTI_FULL)
```

```python
import contextlib
import numpy as np
import concourse.bass as bass
import concourse.mybir as mybir
from concourse.bass_utils import run_bass_kernel_spmd

F32 = mybir.dt.float32
BF16 = mybir.dt.bfloat16
AF = mybir.ActivationFunctionType
ALU = mybir.AluOpType
AX = mybir.AxisListType

L = 2
NK = 8
NF = 22
NSLAB_A = 58
NSLAB_B = 16
EPS = 1e-6
NTI_FULL = 9
GELU_C = 1.5957691216057308


class Buf:
    __slots__ = ("name", "last_w", "readers")

    def __init__(self, name):
        self.name = name
        self.last_w = None
        self.readers = []


class DmaSem:
    def __init__(self, handle, key):
        self.handle = handle
        self.key = key
        self.count = 0


class Sched:
    ENG = ("pe", "act", "dve", "pool", "sp")

    def __init__(self, nc):
        self.nc = nc
        self.items = {e: [] for e in self.ENG}
        self.cnt = {e: 0 for e in self.ENG}
        self.seen = {e: {} for e in self.ENG}
        self.semh = {}
        for e in self.ENG:
            self.semh["p_" + e] = nc.alloc_semaphore(name="p_" + e)

    def dsem(self, name):
        h = self.nc.alloc_semaphore(name=name)
        self.semh[name] = h
        return DmaSem(h, name)

    def _waits(self, eng, reads, writes):
        w = {}

        def add(t, raw):
            if t is None:
                return
            k, v = t
            if k == "p_" + eng and not raw and eng == "pe":
                return
            if w.get(k, 0) < v:
                w[k] = v
        for b in reads:
            add(b.last_w, True)
        for b in writes:
            add(b.last_w, False)
            for t in b.readers:
                add(t, False)
        need = []
        seen = self.seen[eng]
        for k, v in w.items():
            if seen.get(k, 0) < v:
                seen[k] = v
                need.append((k, v))
        return need

    def op(self, eng, meth, kw, reads=(), writes=(), signal=True):
        need = self._waits(eng, reads, writes)
        if signal:
            self.cnt[eng] += 1
            t = ("p_" + eng, self.cnt[eng])
        else:
            t = ("p_" + eng, self.cnt[eng] + 1)
        self.items[eng].append((need, (meth, kw), ("p_" + eng, 1) if signal else None))
        for b in reads:
            if len(b.readers) > 64:
                b.readers = _compress(b.readers)
            b.readers.append(t)
        for b in writes:
            b.last_w = t
            b.readers = []
        return t

    def dma(self, eng, kw, sem, reads=(), writes=()):
        need = self._waits(eng, reads, writes)
        sem.count += 16
        t = (sem.key, sem.count)
        self.items[eng].append((need, ("dma_start", kw), (sem.key, 16)))
        for b in reads:
            b.readers.append(t)
        for b in writes:
            b.last_w = t
            b.readers = []
        return t

    def wait_all(self, eng, tickets):
        need = []
        seen = self.seen[eng]
        for (k, v) in _compress(tickets):
            if seen.get(k, 0) < v:
                seen[k] = v
                need.append((k, v))
        self.items[eng].append((need, None, None))

    def emit(self, block):
        def mk(e):
            items = self.items[e]

            def body(engh):
                for need, fn, sig in items:
                    for k, v in need:
                        engh.wait_ge(self.semh[k], v)
                    if fn is not None:
                        ins = getattr(engh, fn[0])(**fn[1])
                        if sig is not None:
                            ins.then_inc(self.semh[sig[0]], sig[1])
            return body
        block.tensor(mk("pe"))
        block.scalar(mk("act"))
        block.vector(mk("dve"))
        block.gpsimd(mk("pool"))
        block.sync(mk("sp"))


class _Stop(Exception):
    pass


def _compress(ts):
    m = {}
    for k, v in ts:
        if m.get(k, 0) < v:
            m[k] = v
    return list(m.items())


def build_program(NTI, dbg=None):
    nc = bass.Bass("TRN2", target_bir_lowering=False)
    S = Sched(nc)
    dbg = dbg or {}

    def din(name, shape):
        return nc.dram_tensor(name, shape, F32, kind="ExternalInput").ap()

    def dout(name, shape):
        return nc.dram_tensor(name, shape, F32, kind="ExternalOutput").ap()

    xp = din("xp", [NTI * 512, 1024])
    xs = din("xs", [128, 1024])
    cconv = din("cconv", [L, 2, 30, 256])
    ck = din("ck", [L, 2, 512, 512])
    cv = din("cv", [L, 2, 512, 512])
    wA = din("wA", [L * NSLAB_A, 128, 2048])
    wB = din("wB", [L * NSLAB_B, 128, 2816])
    gains_d = din("gains", [128, L * 3 * 8])
    convp_d = din("convp", [128, L * 2 * 34])
    tokp_d = din("tokp", [128, L * 640])
    gbias_d = din("gbias", [128, L * 2 * 128])
    gws_d = din("gws", [L * 4, 128, 128])
    chb_d = din("chb", [128, L * 8])
    biasT_d = din("biasT", [L, 128, 8 * 3 * 128])
    ident_d = din("ident", [128, 128])
    triu_d = din("triu", [128, 128])
    wAb = nc.dram_tensor("wAb", [L * NSLAB_A, 128, 2048], BF16, kind="Internal").ap()
    wBb = nc.dram_tensor("wBb", [L * NSLAB_B, 128, 2816], BF16, kind="Internal").ap()
    yp = dout("yp", [NTI * 512, 1024])
    ys = dout("ys", [128, 1024])
    pconv = dout("pconv", [L, 30, 256])
    pk = dout("pk", [L, 512, 512])
    pv = dout("pv", [L, 512, 512])
    sconv = dout("sconv", [L, 2, 30, 256])
    sk = dout("sk", [L, 2, 64, 512])
    sv = dout("sv", [L, 2, 64, 512])
    sgv = dout("sgv", [L, 2, 64, 256])

    es = contextlib.ExitStack()
    with es:
        def T(name, shape, dt):
            return es.enter_context(nc.sbuf_tensor(name, shape, dt))

        xres = T("xres", [128, NK, 512], F32)
        hT = T("hT", [128, NK, 512], BF16)
        mixT = hT
        sqb = hT
        aT = T("aT", [128, NF, 512], BF16)
        sqc = T("sqc", [128, 2, 512], BF16)
        NSA, NSB = 4, 3
        ringA = T("ringA", [128, NSA, 2048], BF16)
        ringB = T("ringB", [128, NSB, 2816], BF16)
        kT = [T(f"kT{l}", [128, 4, 1024], BF16) for l in range(L)]
        Vt = [T(f"V{l}", [128, 8, 512], BF16) for l in range(L)]
        kTs = kT[0]
        Vs = Vt[0]
        Vs_new = T("Vs_new", [128, 2, 512], BF16)
        glu = [T(f"glu{l}", [128, 2, 542], F32) for l in range(L)]
        gluS = T("gluS", [128, 2, 2, 94], F32)
        cacc = T("cacc", [128, 2, 512], F32)
        gluB = T("gluB", [128, 2, 542], BF16)
        diag = T("diag", [128, 31, 128], BF16)
        qTz = T("qTz", [128, 2, 4, 512], BF16)
        uT = T("uT", [128, 2, 512], F32)
        zaT = T("zaT", [128, 2, 512], F32)
        csq = zaT
        vgb = T("vgb", [128, 4, 256], BF16)
        biasb = T("biasb", [128, 8 * 3 * 128], BF16)
        NR = 4
        sgb = T("sgb", [128, 2, 512], F32)
        st512 = T("st512", [128, NR, 512], F32)
        NTK = 4
        tk256 = T("tk256", [128, NTK, 256], F32)
        tkb = T("tkb", [128, 4, 256], BF16)
        small = T("small", [128, 8, 8], F32)
        bnst = T("bnst", [128, 4, 6], F32)
        PT = T("PT", [128, 3, 5, 128], BF16)
        xin = T("xin", [128, 1, 1024], F32)
        xout = T("xout", [128, 1, 1024], F32)
        cks = xin[:, 0, :].bitcast(BF16).rearrange("p (t f) -> p t f", f=512)
        identf = T("identf", [128, 128], F32)
        identb = T("identb", [128, 128], BF16)
        onesb = T("onesb", [128, 128], BF16)
        onesf = T("onesf", [128, 128], F32)
        epsT = T("epsT", [128, 1], F32)
        triuT = T("triuT", [128, 128], F32)
        gainsT = T("gainsT", [128, L * 3 * 8], F32)
        convpT = T("convpT", [128, L * 2 * 34], F32)
        tokpT = T("tokpT", [128, L * 640], F32)
        gbiasT = T("gbiasT", [128, L * 2 * 128], F32)
        chbT = T("chbT", [128, L * 8], F32)
        gwsin = T("gwsin", [128, 128], F32)
        WsT = T("WsT", [128, L * 4, 128], BF16)

        banks = [es.enter_context(nc.psum_tensor(f"bank{i}", [128, 512], F32)) for i in range(8)]
        Bbank = [Buf(f"bank{i}") for i in range(8)]
        bregs = [[Bbank[i]] for i in range(8)]

        def regs(b, c0, n):
            return [Bbank[b]]
        pools = {"a": [0, 1], "b": [2, 3], "c": [4, 5], "d": [6, 7], "w": [0, 1, 2, 3, 4, 5]}
        pctr = {k: 0 for k in pools}

        def nextbank(pool):
            b = pools[pool][pctr[pool] % len(pools[pool])]
            pctr[pool] += 1
            return b
        Bx = [Buf(f"xres{k}") for k in range(NK)]
        Bh = [Buf(f"hT{k}") for k in range(NK)]
        Bm = Bh
        Ba = [Buf(f"aT{j}") for j in range(NF)]
        Bsqc = [Buf("sqc0"), Buf("sqc1")]
        BrA = [Buf(f"ringA{i}") for i in range(NSA)]
        BrB = [Buf(f"ringB{i}") for i in range(NSB)]
        BkT = [[Buf(f"kT{l}_{t}") for t in range(8)] for l in range(L)]
        BV = [[Buf(f"V{l}_{t}") for t in range(8)] for l in range(L)]
        BkTs = BkT[0]
        BVs = BV[0]
        BVn = [Buf("Vn0"), Buf("Vn1")]
        Bglu = [Buf(f"glu{l}") for l in range(L)]
        BgluS = [Buf(f"gluS{s}") for s in range(2)]
        Bcacc = [Buf("cacc0"), Buf("cacc1")]
        BgluB = [Buf("gluB0"), Buf("gluB1")]
        Bdiag = [Buf(f"diag{w}") for w in range(31)]
        BqT = [Buf(f"qT{g}") for g in range(4)]
        BuT = Buf("uT")
        Bza = Buf("zaT")
        Bcsq = Bza
        Bvgb = [Buf(f"vgb{g}") for g in range(4)]
        Bbias = Buf("biasb")
        Bsg = [Buf("sg0"), Buf("sg1")]
        Bst = [Buf(f"st{i}") for i in range(NR)]
        Btk = [Buf(f"tk{i}") for i in range(NTK)]
        Btkb = [Buf(f"tkb{i}") for i in range(4)]
        Bsmall = [Buf(f"small{i}") for i in range(8)]
        Bbn = [Buf(f"bn{i}") for i in range(4)]
        BPT = [Buf("PT0"), Buf("PT1"), Buf("PT2")]
        Bxin = [Buf("xin0")]
        Bxout = [Buf("xout0")]
        Bcks = Bxin[0]
        Bconst = Buf("const")
        Bgwsin = Buf("gwsin")
        BWs = Buf("WsT")
        rot = {}
        useq = [0]

        def nxt(name, n):
            i = rot.get(name, 0)
            rot[name] = i + 1
            return i % n

        semA = [S.dsem(f"semA{i}") for i in range(NSA)]
        semB = [S.dsem(f"semB{i}") for i in range(NSB)]
        sem_c = S.dsem("sem_c")
        sem_xin = [S.dsem("sem_xin0")]
        sem_out = [S.dsem("sem_out0")]
        dsems = {}

        def dsem_for(name):
            if name not in dsems:
                dsems[name] = S.dsem("ds_" + name)
            return dsems[name]
        out_tickets = []

        ntile = NTI + 1
        seqA = [l * NSLAB_A + i for _ in range(ntile) for l in range(L) for i in range(NSLAB_A)]
        seqB = [l * NSLAB_B + i for _ in range(ntile) for l in range(L) for i in range(NSLAB_B)]
        stA = {"iss": 0, "con": 0}
        stB = {"iss": 0, "con": 0}

        semWA = [S.dsem(f"semWA{i}") for i in range(NSA)]
        semWB = [S.dsem(f"semWB{i}") for i in range(NSB)]
        BdA = [Buf(f"wAb{i}") for i in range(L * NSLAB_A)]
        BdB = [Buf(f"wBb{i}") for i in range(L * NSLAB_B)]
        wb_on = not dbg.get("no_wb")

        def prefetchA():
            while stA["iss"] < stA["con"] + NSA and stA["iss"] < len(seqA):
                i = stA["iss"]
                s = i % NSA
                idx = seqA[i]
                if i < L * NSLAB_A or not wb_on:
                    S.dma("pool", dict(out=ringA[:, s, :], in_=wA[idx]), semA[s], writes=[BrA[s]])
                    if wb_on:
                        S.dma("sp", dict(out=wAb[idx], in_=ringA[:, s, :]), semWA[s], reads=[BrA[s]], writes=[BdA[idx]])
                else:
                    S.dma("pool", dict(out=ringA[:, s, :], in_=wAb[idx]), semA[s], reads=[BdA[idx]], writes=[BrA[s]])
                stA["iss"] += 1

        def prefetchB():
            while stB["iss"] < stB["con"] + NSB and stB["iss"] < len(seqB):
                i = stB["iss"]
                s = i % NSB
                idx = seqB[i]
                if i < L * NSLAB_B or not wb_on:
                    S.dma("pool", dict(out=ringB[:, s, :], in_=wB[idx]), semB[s], writes=[BrB[s]])
                    if wb_on:
                        S.dma("sp", dict(out=wBb[idx], in_=ringB[:, s, :]), semWB[s], reads=[BrB[s]], writes=[BdB[idx]])
                else:
                    S.dma("pool", dict(out=ringB[:, s, :], in_=wBb[idx]), semB[s], reads=[BdB[idx]], writes=[BrB[s]])
                stB["iss"] += 1

        def acquireA(expect, ahead=0):
            i = stA["con"] + ahead
            assert seqA[i] % NSLAB_A == expect % NSLAB_A and i < stA["iss"], (seqA[i], expect)
            s = i % NSA
            return ringA[:, s, :].rearrange("p (k c) -> p k c", c=256), BrA[s]

        def releaseA():
            stA["con"] += 1
            prefetchA()

        def acquireB():
            i = stB["con"]
            assert i < stB["iss"]
            s = i % NSB
            return ringB[:, s, :].rearrange("p (j c) -> p j c", c=128), BrB[s]

        def releaseB():
            stB["con"] += 1
            prefetchB()

        def mm(out, lhsT, rhs, start, stop, reads, writes, signal):
            S.op("pe", "matmul", dict(out=out, lhsT=lhsT, rhs=rhs, start=start, stop=stop),
                 reads=reads, writes=writes, signal=signal)

        def tr(out, in_, ident_ap, reads, writes, signal):
            S.op("pe", "transpose", dict(out=out, in_=in_, identity=ident_ap),
                 reads=reads, writes=writes, signal=signal)

        def act(out, in_, func, reads, writes, scale=None, bias=None):
            kw = dict(out=out, in_=in_, func=func)
            if scale is not None:
                kw["scale"] = scale
            if bias is not None:
                kw["bias"] = bias
            S.op("act", "activation", kw, reads=reads, writes=writes)

        def dve(meth, kw, reads, writes):
            S.op("dve", meth, kw, reads=reads, writes=writes)

        S.dma("sp", dict(out=identf[:], in_=ident_d[:, :]), sem_c, writes=[Bconst])
        S.dma("sp", dict(out=triuT[:], in_=triu_d[:, :]), sem_c, writes=[Bconst])
        S.dma("sp", dict(out=gainsT[:], in_=gains_d[:, :]), sem_c, writes=[Bconst])
        S.dma("sp", dict(out=convpT[:], in_=convp_d[:, :]), sem_c, writes=[Bconst])
        S.dma("sp", dict(out=tokpT[:], in_=tokp_d[:, :]), sem_c, writes=[Bconst])
        S.dma("sp", dict(out=gbiasT[:], in_=gbias_d[:, :]), sem_c, writes=[Bconst])
        S.dma("sp", dict(out=chbT[:], in_=chb_d[:, :]), sem_c, writes=[Bconst])
        prefetchA()
        prefetchB()
        Bc2 = Buf("const2")
        dve("tensor_copy", dict(out=identb[:], in_=identf[:]), [Bconst], [Bc2])
        dve("memset", dict(ap=onesb[:], constant=1.0), [], [Bc2])
        dve("memset", dict(ap=onesf[:], constant=1.0), [], [Bc2])
        dve("memset", dict(ap=epsT[:], constant=EPS), [], [Bc2])
        for l in range(L):
            dve("memset", dict(ap=glu[l][:, :, 0:30], constant=0.0), [], [Bglu[l]])
        CONST = [Bconst, Bc2]
        for lh in range(L * 4):
            S.dma("sp", dict(out=gwsin[:], in_=gws_d[lh]), dsem_for("gwsin"), writes=[Bgwsin])
            b = nextbank("d")
            tr(banks[b][:, 0:128], gwsin[:], identf[:], [Bgwsin] + CONST, regs(b, 0, 128), True)
            dve("tensor_tensor", dict(out=WsT[:, lh, :], in0=banks[b][:, 0:128], in1=triuT[:], op=ALU.mult),
                regs(b, 0, 128) + CONST, [BWs])

        def gain(l, i, kc):
            c = (l * 3 + i) * 8 + kc
            return gainsT[:, c:c + 1]

        def cpar(l, c, w):
            o = (l * 2 + c) * 34 + w
            return convpT[:, o:o + 1]

        pre = {"bank": None, "pend": None}

        def prestat_begin():
            pre["bank"] = nextbank("d")
            pre["pend"] = None

        def prestat_flush(last):
            if pre["pend"] is not None:
                m, si, ncol = pre["pend"]
                mm(banks[pre["bank"]][:, 0:ncol], onesb[:], sqc[:, si, 0:ncol], m == 0, m == NK - 1,
                   [Bsqc[si]] + CONST, [Bbank[pre["bank"]]], m == NK - 1)
                pre["pend"] = None

        def prestat_chunk(m, ncol):
            prestat_flush(False)
            si = nxt("sqc", 2)
            act(sqc[:, si, 0:ncol], xres[:, m, 0:ncol], AF.Square, [Bx[m]], [Bsqc[si]])
            pre["pend"] = (m, si, ncol)

        def rmsnorm(l, gi, ncol):
            if pre["bank"] is not None:
                prestat_flush(True)
                b = pre["bank"]
                pre["bank"] = None
            else:
                act(sqb[:, :, 0:ncol], xres[:, :, 0:ncol], AF.Square, Bx, Bh)
                b = nextbank("d")
                for kc in range(NK):
                    mm(banks[b][:, 0:ncol], onesb[:], sqb[:, kc, 0:ncol], kc == 0, kc == NK - 1,
                       [Bh[kc]] + CONST, regs(b, 0, ncol), kc == NK - 1)
            i = nxt("st", NR)
            act(st512[:, i, 0:ncol], banks[b][:, 0:ncol], AF.Ln, regs(b, 0, ncol) + CONST, [Bst[i]],
                scale=1.0 / 1024.0, bias=epsT[:, 0:1])
            i2 = nxt("st", NR)
            act(st512[:, i2, 0:ncol], st512[:, i, 0:ncol], AF.Exp, [Bst[i]], [Bst[i2]], scale=-0.5)
            for kc in range(NK):
                dve("scalar_tensor_tensor",
                    dict(out=hT[:, kc, 0:ncol], in0=xres[:, kc, 0:ncol], scalar=gain(l, gi, kc), op0=ALU.mult,
                         in1=st512[:, i2, 0:ncol], op1=ALU.mult),
                    [Bx[kc], Bst[i2]] + CONST, [Bh[kc]])

        def ffn(l, w, ncol, prestat_next=True):
            rmsnorm(l, 0 if w == 0 else 2, ncol)
            base = 0 if w == 0 else 36
            for s in range(11):
                wg, bg = acquireA(base + 2 * s)
                wu, bu = acquireA(base + 2 * s + 1, 1)
                for jj in range(2):
                    j = 2 * s + jj
                    g_b = nextbank("a")
                    u_b = nextbank("b")
                    for kc in range(NK):
                        mm(banks[g_b][:, 0:ncol], wg[:, kc, jj * 128:(jj + 1) * 128], hT[:, kc, 0:ncol],
                           kc == 0, kc == NK - 1, [bg, Bh[kc]], regs(g_b, 0, ncol), kc == NK - 1)
                    for kc in range(NK):
                        mm(banks[u_b][:, 0:ncol], wu[:, kc, jj * 128:(jj + 1) * 128], hT[:, kc, 0:ncol],
                           kc == 0, kc == NK - 1, [bu, Bh[kc]], regs(u_b, 0, ncol), kc == NK - 1)
                    si = nxt("sg", 2)
                    act(sgb[:, si, 0:ncol], banks[g_b][:, 0:ncol], AF.Silu, regs(g_b, 0, ncol), [Bsg[si]])
                    dve("tensor_tensor", dict(out=aT[:, j, 0:ncol], in0=banks[u_b][:, 0:ncol],
                                              in1=sgb[:, si, 0:ncol], op=ALU.mult),
                        regs(u_b, 0, ncol) + [Bsg[si]], [Ba[j]])
                releaseA()
                releaseA()
            if prestat_next:
                prestat_begin()
            for m in range(NK):
                wd, bd = acquireB()
                y_b = nextbank("c")
                for j in range(NF):
                    mm(banks[y_b][:, 0:ncol], wd[:, j, :], aT[:, j, 0:ncol], j == 0, j == NF - 1,
                       [bd, Ba[j]], regs(y_b, 0, ncol), j == NF - 1)
                dve("scalar_tensor_tensor",
                    dict(out=xres[:, m, 0:ncol], in0=banks[y_b][:, 0:ncol], scalar=0.5, op0=ALU.mult,
                         in1=xres[:, m, 0:ncol], op1=ALU.add),
                    regs(y_b, 0, ncol) + [Bx[m]], [Bx[m]])
                if prestat_next:
                    prestat_chunk(m, ncol)
                releaseB()

        def gelu_from_psum(zb, zc0, n, npart, out_ap, out_bufs):
            z = banks[zb][0:npart, zc0:zc0 + n]
            zr = regs(zb, zc0, n)
            i = nxt("st", NR)
            act(st512[0:npart, i, 0:n], z, AF.Square, zr, [Bst[i]])
            dve("tensor_scalar", dict(out=st512[0:npart, i, 0:n], in0=st512[0:npart, i, 0:n], scalar1=0.044715,
                                      scalar2=1.0, op0=ALU.mult, op1=ALU.add), [Bst[i]], [Bst[i]])
            dve("tensor_tensor", dict(out=st512[0:npart, i, 0:n], in0=z, in1=st512[0:npart, i, 0:n], op=ALU.mult),
                zr + [Bst[i]], [Bst[i]])
            act(st512[0:npart, i, 0:n], st512[0:npart, i, 0:n], AF.Sigmoid, [Bst[i]], [Bst[i]], scale=GELU_C)
            dve("tensor_tensor", dict(out=out_ap, in0=z, in1=st512[0:npart, i, 0:n], op=ALU.mult),
                zr + [Bst[i]], out_bufs)

        def stage(k):
            if dbg.get("mixstop") == k:
                raise _Stop()

        def mixing(l, ncol, kind, tile_idx):
            sample = kind == "s"
            groups = [(0, 64), (64, 64)] if sample else [(g * 128, 128) for g in range(4)]
            last_prompt = (not sample) and tile_idx == NTI - 1
            first_prompt = (not sample) and tile_idx == 0
            rmsnorm(l, 1, ncol)
            S.dma("pool", dict(out=biasb[:], in_=biasT_d[l]), dsem_for("biasb"), writes=[Bbias])
            act(biasb[:], biasb[:], AF.Copy, [Bbias], [Bbias], scale=8.0)
            if not sample:
                if not first_prompt:
                    for t in range(4):
                        S.op("act", "activation", dict(out=kT[l][:, :, t * 128:(t + 1) * 128],
                                                       in_=kT[l][:, :, 512 + t * 128:512 + (t + 1) * 128], func=AF.Copy),
                             reads=[BkT[l][4 + t]], writes=[BkT[l][t]])
                        S.op("act", "activation", dict(out=Vt[l][:, t, :], in_=Vt[l][:, 4 + t, :], func=AF.Copy),
                             reads=[BV[l][4 + t]], writes=[BV[l][t]])
                    S.op("act", "activation", dict(out=glu[l][:, :, 0:30], in_=glu[l][:, :, 512:542], func=AF.Copy),
                         reads=[Bglu[l]], writes=[Bglu[l]])
            else:
                for s in range(2):
                    for c in range(2):
                        S.dma("sp", dict(out=gluS[:, s, c, 0:30],
                                         in_=cconv[l, s, :, c * 128:(c + 1) * 128].rearrange("t p -> p t"),
                                         allow_slow_non_contiguous=True), dsem_for(f"gluS{s}"), writes=[BgluS[s]])

            stage(1)
            w0, b0 = acquireA(22 + 0)
            for c in range(2):
                zb = nextbank("a")
                for kc in range(NK):
                    mm(banks[zb][:, 0:ncol], w0[:, kc, c * 128:(c + 1) * 128], hT[:, kc, 0:ncol],
                       kc == 0, kc == NK - 1, [b0, Bh[kc]], regs(zb, 0, ncol), kc == NK - 1)
                act(zaT[:, c, 0:ncol], banks[zb][:, 0:ncol], AF.Copy, regs(zb, 0, ncol), [Bza])
            releaseA()
            w1, b1 = acquireA(22 + 1)
            for c in range(2):
                zb = nextbank("b")
                for kc in range(NK):
                    mm(banks[zb][:, 0:ncol], w1[:, kc, c * 128:(c + 1) * 128], hT[:, kc, 0:ncol],
                       kc == 0, kc == NK - 1, [b1, Bh[kc]], regs(zb, 0, ncol), kc == NK - 1)
                si = nxt("sg", 2)
                act(sgb[:, si, 0:ncol], banks[zb][:, 0:ncol], AF.Sigmoid, regs(zb, 0, ncol), [Bsg[si]])
                if not sample:
                    dve("tensor_tensor", dict(out=glu[l][:, c, 30:542], in0=zaT[:, c, 0:512], in1=sgb[:, si, 0:512],
                                              op=ALU.mult), [Bza, Bsg[si]], [Bglu[l]])
                else:
                    for s in range(2):
                        dve("tensor_tensor", dict(out=gluS[:, s, c, 30:94], in0=zaT[:, c, s * 64:(s + 1) * 64],
                                                  in1=sgb[:, si, s * 64:(s + 1) * 64], op=ALU.mult),
                            [Bza, Bsg[si]], [BgluS[s]])
            releaseA()
            if last_prompt:
                for c in range(2):
                    out_tickets.append(S.dma("sp", dict(out=pconv[l, :, c * 128:(c + 1) * 128].rearrange("t p -> p t"),
                                                        in_=glu[l][:, c, 512:542], allow_slow_non_contiguous=True),
                                             dsem_for(f"st_glu{l}"), reads=[Bglu[l]]))
            if sample:
                for s in range(2):
                    for c in range(2):
                        out_tickets.append(S.dma("sp", dict(out=sconv[l, s, :, c * 128:(c + 1) * 128].rearrange("t p -> p t"),
                                                            in_=gluS[:, s, c, 64:94], allow_slow_non_contiguous=True),
                                                 dsem_for(f"st_gluS{s}"), reads=[BgluS[s]]))

            stage(2)
            deferred = []
            segs = [(0, 512, None)] if not sample else [(0, 64, 0), (64, 64, 1)]
            for (c0, n, s) in segs:
                for c in range(2):
                    src_all = glu[l][:, c, 0:30 + n] if s is None else gluS[:, s, c, 0:30 + n]
                    sb = Bglu[l] if s is None else BgluS[s]
                    act(gluB[:, c, 0:30 + n], src_all, AF.Copy, [sb], [BgluB[c]])
                    for w in range(31):
                        if w % 2 == 0:
                            act(diag[:, w, :], identb[:], AF.Copy, CONST, [Bdiag[w]], scale=cpar(l, c, w))
                        else:
                            dve("tensor_scalar", dict(out=diag[:, w, :], in0=identb[:], scalar1=cpar(l, c, w),
                                                      scalar2=None, op0=ALU.mult), CONST, [Bdiag[w]])
                    yb = nextbank("c")
                    for w in range(31):
                        mm(banks[yb][:, 0:n], diag[:, w, :], gluB[:, c, w:w + n], w == 0, w == 30,
                           [Bdiag[w], BgluB[c]], [Bbank[yb]], w == 30)
                    act(cacc[:, c, c0:c0 + n], banks[yb][:, 0:n], AF.Identity, [Bbank[yb]] + CONST, [Bcacc[c]],
                        bias=cpar(l, c, 31))
                act(csq[:, :, c0:c0 + n], cacc[:, :, c0:c0 + n], AF.Square, Bcacc, [Bcsq])
                b1_ = nextbank("c")
                b2_ = nextbank("d")
                for c in range(2):
                    mm(banks[b1_][:, 0:n], onesf[:], cacc[:, c, c0:c0 + n], c == 0, c == 1, [Bcacc[c]] + CONST,
                       regs(b1_, 0, n), c == 1)
                for c in range(2):
                    mm(banks[b2_][:, 0:n], onesf[:], csq[:, c, c0:c0 + n], c == 0, c == 1, [Bcsq] + CONST,
                       regs(b2_, 0, n), c == 1)
                im = nxt("st", NR)
                dve("tensor_scalar", dict(out=st512[:, im, 0:n], in0=banks[b1_][:, 0:n], scalar1=1.0 / 256.0,
                                          scalar2=None, op0=ALU.mult), regs(b1_, 0, n), [Bst[im]])
                iq = nxt("st", NR)
                dve("tensor_tensor", dict(out=st512[:, iq, 0:n], in0=st512[:, im, 0:n], in1=st512[:, im, 0:n],
                                          op=ALU.mult), [Bst[im]], [Bst[iq]])
                dve("scalar_tensor_tensor", dict(out=st512[:, iq, 0:n], in0=banks[b2_][:, 0:n], scalar=1.0 / 256.0,
                                                 op0=ALU.mult, in1=st512[:, iq, 0:n], op1=ALU.subtract),
                    regs(b2_, 0, n) + [Bst[iq]], [Bst[iq]])
                act(st512[:, iq, 0:n], st512[:, iq, 0:n], AF.Sqrt, [Bst[iq]] + CONST, [Bst[iq]], bias=epsT[:, 0:1])
                dve("reciprocal", dict(out=st512[:, iq, 0:n], in_=st512[:, iq, 0:n]), [Bst[iq]], [Bst[iq]])
                for c in range(2):
                    dve("tensor_tensor", dict(out=cacc[:, c, c0:c0 + n], in0=cacc[:, c, c0:c0 + n], in1=st512[:, im, 0:n],
                                              op=ALU.subtract), [Bcacc[c], Bst[im]], [Bcacc[c]])
                    dve("tensor_tensor", dict(out=cacc[:, c, c0:c0 + n], in0=cacc[:, c, c0:c0 + n], in1=st512[:, iq, 0:n],
                                              op=ALU.mult), [Bcacc[c], Bst[iq]], [Bcacc[c]])
                    deferred.append((c, c0, n))

            stage(3)
            w2, b2 = acquireA(22 + 2)
            for c in range(2):
                zb = nextbank("a")
                for kc in range(NK):
                    mm(banks[zb][:, 0:ncol], w2[:, kc, c * 128:(c + 1) * 128], hT[:, kc, 0:ncol],
                       kc == 0, kc == NK - 1, [b2, Bh[kc]], regs(zb, 0, ncol), kc == NK - 1)
                gelu_from_psum(zb, 0, ncol, 128, uT[:, c, 0:ncol], [BuT])
            releaseA()

            stage(4)
            tp = l * 640
            ng = len(groups)

            def tok_matmuls(ws_, bs_):
                zs = []
                for gi, (g0, n) in enumerate(groups):
                    zb = nextbank("w")
                    for kc in range(NK):
                        mm(banks[zb][0:n, 0:256], hT[:, kc, g0:g0 + n], ws_[:, kc, :], kc == 0, kc == NK - 1,
                           [bs_, Bh[kc]], [Bbank[zb]], kc == NK - 1)
                    zs.append(zb)
                return zs

            w3, b3 = acquireA(22 + 3)
            zs = tok_matmuls(w3, b3)
            releaseA()
            G = list(enumerate(groups))

            def zz(gi):
                return banks[zs[gi]][0:groups[gi][1], 0:256]
            for gi, (g0, n) in G:
                act(tk256[0:n, gi, :], zz(gi), AF.Square, [Bbank[zs[gi]]], [Btk[gi]])
            for gi, (g0, n) in G:
                dve("tensor_scalar", dict(out=tk256[0:n, gi, :], in0=tk256[0:n, gi, :], scalar1=0.044715, scalar2=1.0,
                                          op0=ALU.mult, op1=ALU.add), [Btk[gi]], [Btk[gi]])
            for gi, (g0, n) in G:
                dve("tensor_tensor", dict(out=tk256[0:n, gi, :], in0=zz(gi), in1=tk256[0:n, gi, :], op=ALU.mult),
                    [Bbank[zs[gi]], Btk[gi]], [Btk[gi]])
            for gi, (g0, n) in G:
                act(tk256[0:n, gi, :], tk256[0:n, gi, :], AF.Sigmoid, [Btk[gi]], [Btk[gi]], scale=GELU_C)
            for gi, (g0, n) in G:
                dve("tensor_tensor", dict(out=tk256[0:n, gi, :], in0=zz(gi), in1=tk256[0:n, gi, :], op=ALU.mult),
                    [Bbank[zs[gi]], Btk[gi]], [Btk[gi]])
            for gi, (g0, n) in G:
                dve("bn_stats", dict(out=bnst[0:n, gi, :], in_=tk256[0:n, gi, :]), [Btk[gi]], [Bbn[gi]])
            for gi, (g0, n) in G:
                dve("bn_aggr", dict(out=small[0:n, gi, 0:2], in_=bnst[0:n, gi, :]), [Bbn[gi]], [Bsmall[gi]])
            for gi, (g0, n) in G:
                act(small[0:n, gi, 2:3], small[0:n, gi, 1:2], AF.Sqrt, [Bsmall[gi]] + CONST, [Bsmall[gi]],
                    bias=epsT[0:n, 0:1])
            for gi, (g0, n) in G:
                dve("reciprocal", dict(out=small[0:n, gi, 3:4], in_=small[0:n, gi, 2:3]), [Bsmall[gi]], [Bsmall[gi]])
            for gi, (g0, n) in G:
                dve("tensor_scalar", dict(out=tk256[0:n, gi, :], in0=tk256[0:n, gi, :], scalar1=small[0:n, gi, 0:1],
                                          scalar2=small[0:n, gi, 3:4], op0=ALU.subtract, op1=ALU.mult),
                    [Btk[gi], Bsmall[gi]], [Btk[gi]])
            for gi, (g0, n) in G:
                dve("tensor_tensor", dict(out=tk256[0:n, gi, :], in0=tk256[0:n, gi, :], in1=tokpT[0:n, tp:tp + 256],
                                          op=ALU.mult), [Btk[gi]] + CONST, [Btk[gi]])
            for gi, (g0, n) in G:
                dve("tensor_tensor", dict(out=tk256[0:n, gi, :], in0=tk256[0:n, gi, :],
                                          in1=tokpT[0:n, tp + 256:tp + 512], op=ALU.add), [Btk[gi]] + CONST, [Btk[gi]])
            for gi, (g0, n) in G:
                act(vgb[0:n, gi, :], tk256[0:n, gi, :], AF.Copy, [Btk[gi]], [Bvgb[gi]])
                if sample:
                    out_tickets.append(S.dma("sp", dict(out=sgv[l, gi], in_=tk256[0:n, gi, :]), dsem_for(f"st_tk{gi}"),
                                             reads=[Btk[gi]]))

            stage(5)
            pending = []

            def flush_pending():
                for f in pending:
                    f()
                del pending[:]
            for si_ in range(6):
                ws_, bs_ = acquireA(22 + 4 + si_)
                which = si_ // 2
                half_s = si_ % 2
                zs = tok_matmuls(ws_, bs_)
                releaseA()
                flush_pending()

                def zz(gi, zs=zs):
                    return banks[zs[gi]][0:groups[gi][1], 0:256]
                if which == 2:
                    for gi, (g0, n) in G:
                        zr = [Bbank[zs[gi]]]
                        if sample:
                            act(Vs_new[0:n, gi, half_s * 256:(half_s + 1) * 256], zz(gi), AF.Copy, zr, [BVn[gi]])
                        else:
                            act(Vt[l][:, 4 + gi, half_s * 256:(half_s + 1) * 256], zz(gi), AF.Copy, zr, [BV[l][4 + gi]])
                        if sample or last_prompt:
                            act(tk256[0:n, gi, :], zz(gi), AF.Copy, zr, [Btk[gi]])
                            dst = sv[l, gi, :, half_s * 256:(half_s + 1) * 256] if sample else \
                                pv[l, gi * 128:(gi + 1) * 128, half_s * 256:(half_s + 1) * 256]
                            out_tickets.append(S.dma("sp", dict(out=dst, in_=tk256[0:n, gi, :]), dsem_for(f"st_tk{gi}"),
                                                     reads=[Btk[gi]]))
                    continue
                for gi, (g0, n) in G:
                    act(tk256[0:n, gi, :], zz(gi), AF.Square, [Bbank[zs[gi]]], [Btk[gi]])
                for gi, (g0, n) in G:
                    dve("tensor_reduce", dict(out=small[0:n, gi, 0:4],
                                              in_=tk256[0:n, gi, :].rearrange("p (h d) -> p h d", d=64),
                                              axis=AX.X, op=ALU.add), [Btk[gi]], [Bsmall[gi]])
                for gi, (g0, n) in G:
                    act(small[0:n, gi, 0:4], small[0:n, gi, 0:4], AF.Sqrt, [Bsmall[gi]] + CONST, [Bsmall[gi]],
                        scale=1.0 / 64.0, bias=epsT[0:n, 0:1])
                for gi, (g0, n) in G:
                    dve("reciprocal", dict(out=small[0:n, gi, 4:8], in_=small[0:n, gi, 0:4]), [Bsmall[gi]], [Bsmall[gi]])
                for gi, (g0, n) in G:
                    dve("tensor_tensor", dict(out=tk256[0:n, gi, :].rearrange("p (h d) -> p h d", d=64),
                                              in0=zz(gi).rearrange("p (h d) -> p h d", d=64),
                                              in1=small[0:n, gi, 4:8].unsqueeze(2).to_broadcast([n, 4, 64]),
                                              op=ALU.mult), [Bbank[zs[gi]], Bsmall[gi]], [Btk[gi]])
                go = tp + 512 + which * 64
                for gi, (g0, n) in G:
                    dve("tensor_tensor", dict(out=tk256[0:n, gi, :].rearrange("p (h d) -> p h d", d=64),
                                              in0=tk256[0:n, gi, :].rearrange("p (h d) -> p h d", d=64),
                                              in1=tokpT[0:n, go:go + 64].unsqueeze(1).to_broadcast([n, 4, 64]),
                                              op=ALU.mult), [Btk[gi]] + CONST, [Btk[gi]])
                for gi, (g0, n) in G:
                    act(tkb[0:n, gi, :], tk256[0:n, gi, :], AF.Copy, [Btk[gi]], [Btkb[gi]])
                    if which == 1 and (sample or last_prompt):
                        dst = sk[l, gi, :, half_s * 256:(half_s + 1) * 256] if sample else \
                            pk[l, gi * 128:(gi + 1) * 128, half_s * 256:(half_s + 1) * 256]
                        out_tickets.append(S.dma("sp", dict(out=dst, in_=tk256[0:n, gi, :]), dsem_for(f"st_tk{gi}"),
                                                 reads=[Btk[gi]]))

                def transposes(which=which, half_s=half_s):
                    for gi, (g0, n) in G:
                        pb = nextbank("d")
                        pview = banks[pb][:, 0:128].bitcast(BF16)
                        for cc in range(2):
                            tr(pview[:, cc * 128:cc * 128 + n], tkb[0:n, gi, cc * 128:(cc + 1) * 128], identb[0:n, 0:n],
                               [Btkb[gi]] + CONST, [Bbank[pb]], cc == 1)
                        pv3 = pview.rearrange("p (c t) -> p c t", t=128)[:, :, 0:n]
                        ch0 = half_s * 2
                        if which == 0:
                            qb_ = [BqT[gi if not sample else 0]]
                            act(qTz[0:64, 0, ch0:ch0 + 2, g0:g0 + n], pv3[0:64], AF.Copy, [Bbank[pb]], qb_)
                            act(qTz[64:128, 1, ch0:ch0 + 2, g0:g0 + n], pv3[64:128], AF.Copy, [Bbank[pb]], qb_)
                        elif not sample:
                            act(kT[l][:, ch0:ch0 + 2, 512 + g0:512 + g0 + n], pv3, AF.Copy, [Bbank[pb]],
                                [BkT[l][4 + gi]])
                        else:
                            act(kTs_new[:, gi, ch0:ch0 + 2, 0:64], pv3, AF.Copy, [Bbank[pb]], [BkTn[gi]])
                if not dbg.get("no_tr"):
                    pending.append(transposes)
            flush_pending()

            stage(6)
            for (c, c0, n) in deferred:
                act(mixT[:, c, c0:c0 + n], cacc[:, c, c0:c0 + n], AF.Silu, [Bcacc[c]] + CONST, [Bm[c]],
                    scale=cpar(l, c, 32), bias=cpar(l, c, 33))

            for gi, (g0, n) in enumerate(groups):
                for c in range(2):
                    zb = nextbank("c")
                    for half in range(2):
                        h = 2 * c + half
                        mm(banks[zb][half * 64:(half + 1) * 64, 0:n], vgb[0:n, gi, h * 64:(h + 1) * 64],
                           WsT[0:n, l * 4 + h, 0:n], True, True, [Bvgb[gi], BWs], regs(zb, 0, n), half == 1)
                    i = nxt("st", NR)
                    go = (l * 2 + c) * 128
                    dve("tensor_tensor", dict(out=st512[:, i, 0:n], in0=banks[zb][:, 0:n], in1=gbiasT[:, go:go + n],
                                              op=ALU.add), regs(zb, 0, n) + CONST, [Bst[i]])
                    dve("tensor_tensor", dict(out=mixT[:, 2 + c, g0:g0 + n], in0=st512[:, i, 0:n],
                                              in1=uT[:, c, g0:g0 + n], op=ALU.mult), [Bst[i], BuT], [Bm[2 + c]])

            stage(7)
            MSLOT = {0: 0, 3: 1, 4: 2}
            CSLOT = {1: 0, 2: 1}

            def attn_unit_scores(u):
                par = u["seq"] % 3
                mb, cb = 2 * par, 2 * par + 1
                nq = u["nq"]
                keys = u["keys"]
                h = u["h"]
                for idx, (k_ap, v_ap, kb, vb, j) in enumerate(keys):
                    last = idx == len(keys) - 1
                    if j in MSLOT:
                        b, col = mb, MSLOT[j] * 128
                        bo = (h * 3 + MSLOT[j]) * 128
                        mm(banks[b][:, col:col + nq], k_ap, u["q"], True, False, [kb, u["qb"]], [Bbank[b]], False)
                        mm(banks[b][:, col:col + nq], identb[:], biasb[:, bo:bo + nq], False, True,
                           [Bbias] + CONST, [Bbank[b]], last)
                    else:
                        b, col = cb, CSLOT[j] * 128
                        mm(banks[b][:, col:col + nq], k_ap, u["q"], True, True, [kb, u["qb"]], [Bbank[b]], last)
                js = [k[4] for k in keys]
                m0 = min([MSLOT[j] for j in js if j in MSLOT])
                cs = [CSLOT[j] for j in js if j in CSLOT]
                act(PT[:, par, m0:3, 0:nq], banks[mb][:, 0:384].rearrange("p (t q) -> p t q", q=128)[:, m0:3, 0:nq],
                    AF.Exp, [Bbank[mb]], [BPT[par]], scale=0.125)
                if cs:
                    c0_ = min(cs)
                    act(PT[:, par, 3 + c0_:5, 0:nq],
                        banks[cb][:, 0:256].rearrange("p (t q) -> p t q", q=128)[:, c0_:2, 0:nq], AF.Exp,
                        [Bbank[cb]] + CONST, [BPT[par]], scale=0.125, bias=chbT[:, l * 8 + h:l * 8 + h + 1])

            def attn_unit_pv(u):
                par = u["seq"] % 3
                hb = (u["h"] % 2) * 64
                nq = u["nq"]
                keys = u["keys"]
                nkeys = len(keys)

                def slot(j):
                    return MSLOT[j] if j in MSLOT else 3 + CSLOT[j]
                for idx, (k_ap, v_ap, kb, vb, j) in enumerate(keys):
                    mm(banks[u["ob"]][hb:hb + 64, 0:nq], v_ap, PT[:, par, slot(j), 0:nq], idx == 0,
                       idx == nkeys - 1, [vb, BPT[par]], [Bbank[u["ob"]]], False)
                for idx, (k_ap, v_ap, kb, vb, j) in enumerate(keys):
                    mm(banks[u["db"]][hb:hb + 64, 128:128 + nq], onesb[:, 0:64], PT[:, par, slot(j), 0:nq], idx == 0,
                       idx == nkeys - 1, [BPT[par]] + CONST, [Bbank[u["db"]]], idx == nkeys - 1)
                if u["h"] % 2 == 1:
                    i = nxt("st", NR)
                    dve("reciprocal", dict(out=st512[:, i, 0:nq], in_=banks[u["db"]][:, 128:128 + nq]),
                        [Bbank[u["db"]]], [Bst[i]])
                    dve("tensor_tensor", dict(out=mixT[:, 4 + u["h"] // 2, u["c0"]:u["c0"] + nq],
                                              in0=banks[u["ob"]][:, 0:nq], in1=st512[:, i, 0:nq], op=ALU.mult),
                        [Bbank[u["ob"]], Bst[i]], [Bm[4 + u["h"] // 2]])

            def run_units(units):
                for i, u in enumerate(units):
                    u["seq"] = useq[0]
                    useq[0] += 1
                    attn_unit_scores(u)
                    if i >= 2:
                        attn_unit_pv(units[i - 2])
                for u in units[-2:]:
                    attn_unit_pv(u)

            if not sample and dbg.get("no_attn_p"):
                pass
            elif sample and dbg.get("no_attn_s"):
                pass
            elif not sample:
                units = []
                for p in range(4):
                    jmin = max(0, 4 - p) if first_prompt else dbg.get("jmin", 0)
                    for c in range(4):
                        ob = db = nextbank("d")
                        for half in range(2):
                            h = 2 * c + half
                            hb = half * 64
                            keys = []
                            for j in range(jmin, 5):
                                kt = p + j
                                keys.append((kT[l][:, c, kt * 128:(kt + 1) * 128],
                                             Vt[l][:, kt, h * 64:(h + 1) * 64], BkT[l][kt], BV[l][kt], j))
                            units.append(dict(h=h, nq=128, c0=p * 128, q=qTz[:, half, c, p * 128:(p + 1) * 128],
                                              qb=BqT[p], keys=keys, ob=ob, db=db))
                run_units(units)
            else:
                allprompt = [b_ for l_ in range(L) for b_ in BkT[l_]] + [b_ for l_ in range(L) for b_ in BV[l_]]
                for s in range(2):
                    S.dma("pool", dict(out=cks, in_=ck[l, s].rearrange("(t p) f -> p t f", p=128)),
                          dsem_for("cks"), writes=[Bcks])
                    S.dma("pool", dict(out=Vs[:, 0:4, :], in_=cv[l, s].rearrange("(t p) f -> p t f", p=128)),
                          dsem_for("Vs"), writes=BVs[0:4] + allprompt)
                    for t in range(4):
                        for c2 in range(2):
                            pb = nextbank("d")
                            pview = banks[pb][:, 0:128].bitcast(BF16)
                            for cc in range(2):
                                tr(pview[:, cc * 128:(cc + 1) * 128], cks[:, t, (c2 * 2 + cc) * 128:(c2 * 2 + cc + 1) * 128],
                                   identb[:], [Bcks] + CONST, regs(pb, 0, 128), cc == 1)
                            act(kTs[:, c2 * 2:c2 * 2 + 2, t * 128:(t + 1) * 128],
                                pview.rearrange("p (c t) -> p c t", t=128), AF.Copy, regs(pb, 0, 128),
                                [BkTs[t]] + allprompt)
                    units = []
                    for c in range(4):
                        ob = db = nextbank("d")
                        for half in range(2):
                            h = 2 * c + half
                            hb = half * 64
                            keys = []
                            for j in range(4):
                                keys.append((kTs[:, c, j * 128:(j + 1) * 128], Vs[:, j, h * 64:(h + 1) * 64],
                                             BkTs[j], BVs[j], j))
                            keys.append((kTs_new[:, s, c, :], Vs_new[:, s, h * 64:(h + 1) * 64],
                                         BkTn[s], BVn[s], 4))
                            units.append(dict(h=h, nq=64, c0=s * 64, q=qTz[:, half, c, s * 64:(s + 1) * 64],
                                              qb=BqT[0], keys=keys, ob=ob, db=db))
                    run_units(units)

            stage(8)
            prestat_begin()
            for s4 in range(4):
                wo, bo_ = acquireA(22 + 10 + s4)
                for mm_ in range(2):
                    m = 2 * s4 + mm_
                    yb = nextbank("a" if mm_ == 0 else "b")
                    for kc in range(NK):
                        mm(banks[yb][:, 0:ncol], wo[:, kc, mm_ * 128:(mm_ + 1) * 128], mixT[:, kc, 0:ncol],
                           kc == 0, kc == NK - 1, [bo_, Bm[kc]], regs(yb, 0, ncol), kc == NK - 1)
                    dve("tensor_tensor", dict(out=xres[:, m, 0:ncol], in0=banks[yb][:, 0:ncol], in1=xres[:, m, 0:ncol],
                                              op=ALU.add), regs(yb, 0, ncol) + [Bx[m]], [Bx[m]])
                    prestat_chunk(m, ncol)
                releaseA()

        kTs_new = T("kTs_new", [128, 2, 4, 128], BF16)
        BkTn = [Buf("kTn0"), Buf("kTn1")]
        dve("memset", dict(ap=kTs_new[:], constant=0.0), [], BkTn)
        dve("memset", dict(ap=qTz[:], constant=0.0), [], BqT)
        dve("memset", dict(ap=Vs_new[:], constant=0.0), [], BVn)

        def load_tile(kind, tile_idx):
            groups = [(0, 64), (64, 64)] if kind == "s" else [(g * 128, 128) for g in range(4)]
            for gi, (g0, n) in enumerate(groups):
                xi = nxt("xin", 1)
                src = xs[g0:g0 + n, :] if kind == "s" else xp[tile_idx * 512 + g0:tile_idx * 512 + g0 + n, :]
                S.dma("sp", dict(out=xin[0:n, xi, :], in_=src), sem_xin[xi], writes=[Bxin[xi]])
                for hh in range(2):
                    b = nextbank("c" if hh == 0 else "d")
                    for q4 in range(4):
                        kc = hh * 4 + q4
                        tr(banks[b][:, q4 * 128:q4 * 128 + n], xin[0:n, xi, kc * 128:(kc + 1) * 128], identf[0:n, 0:n],
                           [Bxin[xi]] + CONST, [Bbank[b]], q4 == 3)
                    src3 = banks[b][:, :].rearrange("p (c t) -> p c t", t=128)[:, :, 0:n]
                    if hh == 0:
                        act(xres[:, 0:4, g0:g0 + n], src3, AF.Copy, bregs[b], Bx[0:4])
                    else:
                        dve("tensor_copy", dict(out=xres[:, 4:8, g0:g0 + n], in_=src3), bregs[b], Bx[4:8])

        def store_tile(kind, tile_idx):
            groups = [(0, 64), (64, 64)] if kind == "s" else [(g * 128, 128) for g in range(4)]
            for gi, (g0, n) in enumerate(groups):
                xo = nxt("xout", 1)
                for hh in range(2):
                    b = nextbank("c" if hh == 0 else "d")
                    for q4 in range(4):
                        kc = hh * 4 + q4
                        tr(banks[b][0:n, q4 * 128:(q4 + 1) * 128], xres[:, kc, g0:g0 + n], identf[:, :],
                           [Bx[kc]] + CONST, [Bbank[b]], q4 == 3)
                    if hh == 0:
                        act(xout[0:n, xo, 0:512], banks[b][0:n, :], AF.Copy, bregs[b], [Bxout[xo]])
                    else:
                        dve("tensor_copy", dict(out=xout[0:n, xo, 512:1024], in_=banks[b][0:n, :]), bregs[b], [Bxout[xo]])
                dst = ys[g0:g0 + n, :] if kind == "s" else yp[tile_idx * 512 + g0:tile_idx * 512 + g0 + n, :]
                out_tickets.append(S.dma("sp", dict(out=dst, in_=xout[0:n, xo, :]), sem_out[xo], reads=[Bxout[xo]]))

        tiles = [("p", t) for t in range(NTI)] + [("s", 0)]
        for (kind, ti) in tiles:
            ncol = 128 if kind == "s" else 512
            load_tile(kind, ti)
            for l in range(dbg.get("layers", L)):
                if "ffn1" in dbg.get("phases", ("ffn1", "mix", "ffn2")):
                    ffn(l, 0, ncol)
                else:
                    stA["con"] += 22; stB["con"] += 8; prefetchA(); prefetchB()
                if "mix" in dbg.get("phases", ("ffn1", "mix", "ffn2")):
                    a0 = stA["con"]
                    try:
                        mixing(l, ncol, kind, ti)
                    except _Stop:
                        stA["con"] = a0 + 14
                        prefetchA()
                else:
                    stA["con"] += 14; prefetchA()
                if "ffn2" in dbg.get("phases", ("ffn1", "mix", "ffn2")):
                    ffn(l, 1, ncol, prestat_next=(l < L - 1))
                else:
                    stA["con"] += 22; stB["con"] += 8; prefetchA(); prefetchB()
            for l in range(dbg.get("layers", L), L):
                stA["con"] += NSLAB_A; stB["con"] += NSLAB_B; prefetchA(); prefetchB()
            store_tile(kind, ti)

        S.wait_all("sp", out_tickets)
        with nc.Block() as block:
            S.emit(block)
    return nc


def _slabA(w):
    C = w.shape[1]
    return np.ascontiguousarray(w.reshape(8, 128, C // 256, 256).transpose(2, 1, 0, 3)).reshape(C // 256, 128, 2048)


def _slabB(w):
    return np.ascontiguousarray(w.reshape(22, 128, 8, 128).transpose(2, 1, 0, 3)).reshape(8, 128, 2816)


def prepare_shared(inp):
    f = np.float32
    wA = np.empty((L * NSLAB_A, 128, 2048), f)
    wB = np.empty((L * NSLAB_B, 128, 2816), f)
    for l in range(L):
        o = l * NSLAB_A
        for w, base in ((0, 0), (1, 36)):
            g = _slabA(np.asarray(inp[f"ffn{w + 1}_w_gate"][l]))
            u = _slabA(np.asarray(inp[f"ffn{w + 1}_w_up"][l]))
            for s in range(11):
                wA[o + base + 2 * s] = g[s]
                wA[o + base + 2 * s + 1] = u[s]
            wB[l * NSLAB_B + w * 8:l * NSLAB_B + w * 8 + 8] = _slabB(np.asarray(inp[f"ffn{w + 1}_w_down"][l]))
        wA[o + 22:o + 32] = _slabA(np.asarray(inp["w_in"][l]))
        wA[o + 32:o + 36] = _slabA(np.asarray(inp["w_out"][l]))
    gains = np.empty((128, L, 3, 8), f)
    for l in range(L):
        for i, nm in enumerate(("ffn1_norm", "mix_norm", "ffn2_norm")):
            gains[:, l, i, :] = np.asarray(inp[nm][l]).reshape(8, 128).T
    convp = np.empty((128, L, 2, 34), f)
    for l in range(L):
        cw = np.asarray(inp["conv_w"][l])
        for c in range(2):
            convp[:, l, c, 0:31] = cw[:, c * 128:(c + 1) * 128].T
            convp[:, l, c, 31] = np.asarray(inp["conv_b"][l])[c * 128:(c + 1) * 128]
            convp[:, l, c, 32] = np.asarray(inp["conv_ln_g"][l])[c * 128:(c + 1) * 128]
            convp[:, l, c, 33] = np.asarray(inp["conv_ln_b"][l])[c * 128:(c + 1) * 128]
    tokp = np.empty((128, L, 640), f)
    for l in range(L):
        tokp[:, l, 0:256] = np.asarray(inp["gmlp_ln_g"][l])[None, :]
        tokp[:, l, 256:512] = np.asarray(inp["gmlp_ln_b"][l])[None, :]
        tokp[:, l, 512:576] = np.asarray(inp["q_norm"][l])[None, :]
        tokp[:, l, 576:640] = np.asarray(inp["k_norm"][l])[None, :]
    gb = np.asarray(inp["gmlp_b"])
    gbias = np.empty((128, L, 2, 128), f)
    for l in range(L):
        for c in range(2):
            gbias[0:64, l, c, :] = gb[l, 2 * c][None, :]
            gbias[64:128, l, c, :] = gb[l, 2 * c + 1][None, :]
    rb = np.asarray(inp["rel_bias"])
    chb = np.empty((128, L, 8), f)
    chb[:] = rb[None, :, :, 256]
    kl = np.arange(128)[:, None]
    ql = np.arange(128)[None, :]
    biasT = np.empty((L, 128, 8, 3, 128), f)
    for ji, j in enumerate((0, 3, 4)):
        rel = ql - kl + (4 - j) * 128
        idx = np.clip(rel, -128, 128) + 128
        tab = rb[:, :, idx]
        if j == 4:
            msk = (kl >= 64) & (ql < 64)
        elif j == 0:
            msk = (kl < 64) & (ql >= 64)
        else:
            msk = np.zeros((128, 128), bool)
        tab = np.where(msk[None, None], f(-30000.0), tab)
        biasT[:, :, :, ji, :] = tab.transpose(0, 2, 1, 3)
    return dict(
        wA=wA, wB=wB, gains=gains.reshape(128, -1), convp=convp.reshape(128, -1), tokp=tokp.reshape(128, -1),
        gbias=gbias.reshape(128, -1), gws=np.ascontiguousarray(np.asarray(inp["gmlp_ws"], f).reshape(L * 4, 128, 128)),
        chb=chb.reshape(128, -1), biasT=biasT.reshape(L, 128, -1), ident=np.eye(128, dtype=f),
        triu=np.triu(np.ones((128, 128), f)),
    )


def run(inp, NTI):
    SEQ = 512 * (2 * NTI - 2)
    x_prompt = np.asarray(inp["x_prompt"], np.float32)
    x_sample = np.asarray(inp["x_sample"], np.float32)
    B = x_prompt.shape[0]
    assert x_prompt.shape[1] == SEQ and B * 2 == 8 and x_sample.shape[0] == 16
    shared = prepare_shared(inp)
    cconv = np.asarray(inp["cache_conv"], np.float32)
    ck = np.asarray(inp["cache_k"], np.float32).reshape(L, 16, 512, 512)
    cv = np.asarray(inp["cache_v"], np.float32).reshape(L, 16, 512, 512)
    in_maps = []
    for c in range(8):
        b, half = c // 2, c % 2
        t0 = 0 if half == 0 else SEQ - NTI * 512
        m = dict(shared)
        m["xp"] = np.ascontiguousarray(x_prompt[b, t0:t0 + NTI * 512])
        m["xs"] = np.ascontiguousarray(x_sample[2 * c:2 * c + 2].reshape(128, 1024))
        m["cconv"] = np.ascontiguousarray(cconv[:, 2 * c:2 * c + 2])
        m["ck"] = np.ascontiguousarray(ck[:, 2 * c:2 * c + 2])
        m["cv"] = np.ascontiguousarray(cv[:, 2 * c:2 * c + 2])
        in_maps.append(m)
    nc = build_program(NTI)
    res = run_bass_kernel_spmd(nc, in_maps, core_ids=list(range(8)))
    R = res.results
    f = np.float32
    y_prompt = np.empty((B, SEQ, 1024), f)
    y_sample = np.empty((16, 64, 1024), f)
    p_conv = np.empty((L, B, 30, 256), f)
    p_k = np.empty((L, B, 512, 8, 64), f)
    p_v = np.empty((L, B, 512, 8, 64), f)
    s_conv = np.empty((L, 16, 30, 256), f)
    s_k = np.empty((L, 16, 64, 8, 64), f)
    s_v = np.empty((L, 16, 64, 8, 64), f)
    s_gv = np.empty((L, 16, 64, 4, 64), f)
    for c in range(8):
        b, half = c // 2, c % 2
        r = R[c]
        if half == 0:
            y_prompt[b, 0:NTI * 512] = r["yp"]
        else:
            y_prompt[b, NTI * 512:SEQ] = r["yp"][1024:]
            p_conv[:, b] = r["pconv"]
            p_k[:, b] = r["pk"].reshape(L, 512, 8, 64)
            p_v[:, b] = r["pv"].reshape(L, 512, 8, 64)
        y_sample[2 * c:2 * c + 2] = r["ys"].reshape(2, 64, 1024)
        s_conv[:, 2 * c:2 * c + 2] = r["sconv"]
        s_k[:, 2 * c:2 * c + 2] = r["sk"].reshape(L, 2, 64, 8, 64)
        s_v[:, 2 * c:2 * c + 2] = r["sv"].reshape(L, 2, 64, 8, 64)
        s_gv[:, 2 * c:2 * c + 2] = r["sgv"].reshape(L, 2, 64, 4, 64)
    return (y_prompt, y_sample, p_conv, p_k, p_v, s_conv, s_k, s_v, s_gv)


def kernel(**inputs):
    return run(inputs, NTI_FULL)
```
